# Optimizing a Trainium2 kernel written in Bass

```python
import math
import jax, jax.numpy as jnp
from jax import lax
import numpy as np

D_MODEL = 2048
BATCH = 32
SEQ = 256
DEPTH = 2
DEC_BATCH = 8
DEC_SEQ = 1024
PAST_LEN = 512

GRID_W = 64
HEAD_DIM = 128
QBLK = 128
ROPE_BASE = 10000.0
EPS = 1e-6
NEG_INF = -1e30

MLA_HEADS = 4
MLA_Q_RANK = 384
MLA_KV_RANK = 128
MLA_NOPE = 128
MLA_ROPE = 64
MLA_V = 128
SWA_HEADS = 4
SWA_KV_HEADS = 2
SWA_WINDOW = 128
SWA_BLK = 128
NA_HEADS = 4
NA_ROWS = 8
NA_COLS = 16
DIFF_HEADS = 4
DIFF_QK = 64
DIFF_V = 2 * DIFF_QK

N_BRANCH = 4
BRANCH_W = 512
FFN_HIDDEN = ((8 * D_MODEL + 3 * 256 - 1) // (3 * 256)) * 256

PROJ_SIZES = (MLA_Q_RANK, MLA_KV_RANK, MLA_ROPE,
              SWA_HEADS * HEAD_DIM, SWA_KV_HEADS * HEAD_DIM, SWA_KV_HEADS * HEAD_DIM,
              NA_HEADS * HEAD_DIM, NA_HEADS * HEAD_DIM, NA_HEADS * HEAD_DIM,
              DIFF_HEADS * 2 * DIFF_QK, DIFF_HEADS * 2 * DIFF_QK, DIFF_HEADS * DIFF_V)
IN_WIDTH = (MLA_Q_RANK + MLA_KV_RANK + MLA_ROPE + (SWA_HEADS + 2 * SWA_KV_HEADS) * HEAD_DIM
            + 3 * NA_HEADS * HEAD_DIM + DIFF_HEADS * (4 * DIFF_QK + DIFF_V))

kernel_name = 'hybrid_diffusion_prefix_trunk_step'


def split_points():
    pts, acc = [], 0
    for s in PROJ_SIZES[:-1]:
        acc += s
        pts.append(acc)
    return pts


def lambda_init(l):
    return 0.8 - 0.6 * math.exp(-0.3 * l)


def rmsnorm(x, g):
    xf = x.astype(jnp.float32)
    y = xf * lax.rsqrt(jnp.mean(xf * xf, axis=-1, keepdims=True) + EPS)
    return (y * g.astype(jnp.float32)).astype(x.dtype)


def axial_rope(x):
    n, r = x.shape[1], x.shape[-1]
    t = jnp.arange(n)
    row = (t // GRID_W).astype(jnp.float32)
    col = (t % GRID_W).astype(jnp.float32)
    quarter = r // 4
    inv = 1.0 / (ROPE_BASE ** (jnp.arange(quarter, dtype=jnp.float32) / quarter))
    ang = jnp.concatenate([row[:, None] * inv, col[:, None] * inv], axis=-1)
    cos = jnp.cos(ang)[None, :, None, :]
    sin = jnp.sin(ang)[None, :, None, :]
    xf = x.astype(jnp.float32)
    x1, x2 = xf[..., :r // 2], xf[..., r // 2:]
    return jnp.concatenate([x1 * cos - x2 * sin, x1 * sin + x2 * cos], axis=-1).astype(x.dtype)


def attend_dense(q, k, v, scale, sink=None):
    b, nq, h, dq = q.shape
    hk = k.shape[2]
    g = h // hk
    dv = v.shape[-1]
    qb = q.reshape(b, nq // QBLK, QBLK, hk, g, dq).swapaxes(0, 1)

    def block(qi):
        s = jnp.einsum('bqkgd,bskd->bkgqs', qi, k).astype(jnp.float32) * scale
        if sink is None:
            p = jax.nn.softmax(s, axis=-1)
        else:
            sk = jnp.broadcast_to(sink.astype(jnp.float32).reshape(hk, g)[None, :, :, None, None],
                                  s.shape[:-1] + (1,))
            p = jax.nn.softmax(jnp.concatenate([sk, s], axis=-1), axis=-1)[..., 1:]
        return jnp.einsum('bkgqs,bskd->bqkgd', p.astype(v.dtype), v)

    o = lax.map(block, qb)
    return o.swapaxes(0, 1).reshape(b, nq, h, dv)


def mla_attend(q_nope, q_rope, ckv, krope, w_ukv):
    b, nk = ckv.shape[0], ckv.shape[1]
    kv = (ckv @ w_ukv).reshape(b, nk, MLA_HEADS, MLA_NOPE + MLA_V)
    k = jnp.concatenate([kv[..., :MLA_NOPE],
                         jnp.broadcast_to(krope, (b, nk, MLA_HEADS, MLA_ROPE))], axis=-1)
    q = jnp.concatenate([q_nope, q_rope], axis=-1)
    return attend_dense(q, k, kv[..., MLA_NOPE:], (MLA_NOPE + MLA_ROPE) ** -0.5)


def swa_latent(q, k, v, k_ctx, v_ctx, sink):
    b, n, h, d = q.shape
    hk = k.shape[2]
    g = h // hk
    nb = n // SWA_BLK
    nctx = k_ctx.shape[1]
    pad = ((0, 0), (SWA_BLK, SWA_BLK), (0, 0), (0, 0))
    kp = jnp.pad(k, pad)
    vp = jnp.pad(v, pad)
    idx = jnp.arange(nb)[:, None] * SWA_BLK + jnp.arange(3 * SWA_BLK)[None, :]
    kb = kp[:, idx]
    vb = vp[:, idx]
    kpos = idx - SWA_BLK
    qpos = jnp.arange(nb)[:, None] * SWA_BLK + jnp.arange(SWA_BLK)[None, :]
    valid = ((jnp.abs(qpos[:, :, None] - kpos[:, None, :]) <= SWA_WINDOW)
             & (kpos[:, None, :] >= 0) & (kpos[:, None, :] < n))
    qg = q.reshape(b, nb, SWA_BLK, hk, g, d)
    scale = d ** -0.5
    s_loc = jnp.einsum('bnqkgd,bnskd->bnkgqs', qg, kb).astype(jnp.float32) * scale
    s_loc = jnp.where(valid[None, :, None, None, :, :], s_loc, NEG_INF)
    s_ctx = jnp.einsum('bnqkgd,bskd->bnkgqs', qg, k_ctx).astype(jnp.float32) * scale
    sk = jnp.broadcast_to(sink.astype(jnp.float32).reshape(hk, g)[None, None, :, :, None, None],
                          s_ctx.shape[:-1] + (1,))
    p = jax.nn.softmax(jnp.concatenate([sk, s_ctx, s_loc], axis=-1), axis=-1).astype(v.dtype)
    o = (jnp.einsum('bnkgqs,bskd->bnqkgd', p[..., 1:1 + nctx], v_ctx)
         + jnp.einsum('bnkgqs,bnskd->bnqkgd', p[..., 1 + nctx:], vb))
    return o.reshape(b, n, h, d)


def na_latent(q, k, v, k_ctx, v_ctx, rpb):
    b, n, h, d = q.shape
    rows = n // GRID_W
    kh = min(NA_ROWS, rows)
    nctx = k_ctx.shape[1]
    qg = q.reshape(b, rows, GRID_W, h, d)
    kg = k.reshape(b, rows, GRID_W, h, d)
    vg = v.reshape(b, rows, GRID_W, h, d)
    r = jnp.arange(rows)
    rs = jnp.clip(r - kh // 2, 0, rows - kh)
    row_idx = rs[:, None] + jnp.arange(kh)[None, :]
    k_rows = kg[:, row_idx]
    v_rows = vg[:, row_idx]
    cols = jnp.arange(GRID_W)
    cs = jnp.clip(cols - NA_COLS // 2, 0, GRID_W - NA_COLS)
    col_valid = (cols[None, :] >= cs[:, None]) & (cols[None, :] < cs[:, None] + NA_COLS)
    dr = row_idx - r[:, None] + (NA_ROWS - 1)
    dc = jnp.clip(cols[None, :] - cols[:, None] + NA_COLS - 1, 0, 2 * NA_COLS - 2)
    bias = rpb[:, dr[:, None, :, None], dc[None, :, None, :]]
    bias = bias.astype(jnp.float32).transpose(1, 0, 2, 3, 4)[None]
    scale = d ** -0.5
    s_loc = jnp.einsum('brchd,brkwhd->brhckw', qg, k_rows).astype(jnp.float32) * scale + bias
    s_loc = jnp.where(col_valid[:, None, :], s_loc, NEG_INF).reshape(b, rows, h, GRID_W, kh * GRID_W)
    s_ctx = jnp.einsum('brchd,bshd->brhcs', qg, k_ctx).astype(jnp.float32) * scale
    p = jax.nn.softmax(jnp.concatenate([s_ctx, s_loc], axis=-1), axis=-1).astype(v.dtype)
    p_loc = p[..., nctx:].reshape(b, rows, h, GRID_W, kh, GRID_W)
    o = (jnp.einsum('brhcs,bshd->brchd', p[..., :nctx], v_ctx)
         + jnp.einsum('brhckw,brkwhd->brchd', p_loc, v_rows))
    return o.reshape(b, n, h, d)


def diff_attend(q, k, v, P, lam_init):
    b, nq, h, _ = q.shape
    f32 = jnp.float32
    lam = (jnp.exp(jnp.sum(P['diff_lq1'].astype(f32) * P['diff_lk1'].astype(f32)))
           - jnp.exp(jnp.sum(P['diff_lq2'].astype(f32) * P['diff_lk2'].astype(f32))) + lam_init)
    k1, k2 = k[..., :DIFF_QK], k[..., DIFF_QK:]
    scale = DIFF_QK ** -0.5
    qb = q.reshape(b, nq // QBLK, QBLK, h, 2 * DIFF_QK).swapaxes(0, 1)

    def block(qi):
        s1 = jnp.einsum('bqhd,bshd->bhqs', qi[..., :DIFF_QK], k1).astype(f32) * scale
        s2 = jnp.einsum('bqhd,bshd->bhqs', qi[..., DIFF_QK:], k2).astype(f32) * scale
        p = jax.nn.softmax(s1, axis=-1) - lam * jax.nn.softmax(s2, axis=-1)
        return jnp.einsum('bhqs,bshd->bqhd', p.astype(v.dtype), v)

    o = lax.map(block, qb).swapaxes(0, 1).reshape(b, nq, h, DIFF_V)
    return rmsnorm(o, P['diff_subln_g']) * (1.0 - lam_init)


def modulation(cvec, w_ada, b_ada):
    m = (jax.nn.silu(cvec) @ w_ada + b_ada)[:, None, :]
    return jnp.split(m, 6, axis=-1)


def project_mixers(h, P, latent):
    b, n = h.shape[0], h.shape[1]
    proj = h @ P['w_in']
    (q_down, ckv_raw, krope, qb, kb, vb, qc, kc, vc, qd, kd, vd) = jnp.split(proj, split_points(), axis=-1)
    cq = rmsnorm(q_down, P['mla_q_norm_g'])
    qa = (cq @ P['w_mla_uq']).reshape(b, n, MLA_HEADS, MLA_NOPE + MLA_ROPE)
    qa_nope, qa_rope = qa[..., :MLA_NOPE], qa[..., MLA_NOPE:]
    ckv = rmsnorm(ckv_raw, P['mla_kv_norm_g'])
    krope = krope.reshape(b, n, 1, MLA_ROPE)
    qb = qb.reshape(b, n, SWA_HEADS, HEAD_DIM)
    kb = kb.reshape(b, n, SWA_KV_HEADS, HEAD_DIM)
    vb = vb.reshape(b, n, SWA_KV_HEADS, HEAD_DIM)
    qc = qc.reshape(b, n, NA_HEADS, HEAD_DIM)
    kc = kc.reshape(b, n, NA_HEADS, HEAD_DIM)
    vc = vc.reshape(b, n, NA_HEADS, HEAD_DIM)
    qd = qd.reshape(b, n, DIFF_HEADS, 2 * DIFF_QK)
    kd = kd.reshape(b, n, DIFF_HEADS, 2 * DIFF_QK)
    vd = vd.reshape(b, n, DIFF_HEADS, DIFF_V)
    if latent:
        qa_rope = axial_rope(qa_rope)
        krope = axial_rope(krope)
        qb = axial_rope(qb)
        kb = axial_rope(kb)
        qd = axial_rope(qd.reshape(b, n, 2 * DIFF_HEADS, DIFF_QK)).reshape(b, n, DIFF_HEADS, 2 * DIFF_QK)
        kd = axial_rope(kd.reshape(b, n, 2 * DIFF_HEADS, DIFF_QK)).reshape(b, n, DIFF_HEADS, 2 * DIFF_QK)
    return (qa_nope, qa_rope, ckv, krope, qb, kb, vb, qc, kc, vc, qd, kd, vd)


def merge_branches(h, outs, P):
    b, n = h.shape[0], h.shape[1]
    mixed = None
    for i, o in enumerate(outs):
        gate = jax.nn.sigmoid(h @ P['w_mix_gate'][i])
        term = gate * (o.reshape(b, n, BRANCH_W) @ P['w_branch'][i])
        mixed = term if mixed is None else mixed + term
    return mixed @ P['w_out']


def trunk_layer(x, cvec, P, lam_init, ctx_cache):
    sh1, sc1, g1, sh2, sc2, g2 = modulation(cvec, P['w_ada'], P['b_ada'])
    h = rmsnorm(x, P['mix_pre_g']) * (1.0 + sc1) + sh1
    latent = ctx_cache is not None
    (qa_nope, qa_rope, ckv, krope, qb, kb, vb, qc, kc, vc, qd, kd, vd) = project_mixers(h, P, latent)
    own = None
    if latent:
        ckv_c, krope_c, kb_c, vb_c, kc_c, vc_c, kd_c, vd_c = ctx_cache
        o_a = mla_attend(qa_nope, qa_rope, jnp.concatenate([ckv_c, ckv], axis=1),
                         jnp.concatenate([krope_c[:, :, None, :], krope], axis=1), P['w_mla_ukv'])
        o_b = swa_latent(qb, kb, vb, kb_c, vb_c, P['swa_sink'])
        o_c = na_latent(qc, kc, vc, kc_c, vc_c, P['na_rpb'])
        o_d = diff_attend(qd, jnp.concatenate([kd_c, kd], axis=1),
                          jnp.concatenate([vd_c, vd], axis=1), P, lam_init)
    else:
        o_a = mla_attend(qa_nope, qa_rope, ckv, krope, P['w_mla_ukv'])
        o_b = attend_dense(qb, kb, vb, HEAD_DIM ** -0.5, sink=P['swa_sink'])
        o_c = attend_dense(qc, kc, vc, HEAD_DIM ** -0.5)
        o_d = diff_attend(qd, kd, vd, P, lam_init)
        own = (ckv, krope[:, :, 0, :], kb, vb, kc, vc, kd, vd)
    mixed = merge_branches(h, (o_a, o_b, o_c, o_d), P)
    x = x + g1 * rmsnorm(mixed, P['mix_post_g'])
    h2 = rmsnorm(x, P['ffn_pre_g']) * (1.0 + sc2) + sh2
    f = (jax.nn.silu(h2 @ P['w_ffn_gate']) * (h2 @ P['w_ffn_up'])) @ P['w_ffn_down']
    x = x + g2 * rmsnorm(f, P['ffn_post_g'])
    return x, own


def setup_inputs(seed: int = 0) -> dict:
    key = jax.random.key(seed)
    ks = jax.random.split(key, 40)
    f32 = jnp.float32

    def nrm(i, shape, scale=1.0):
        return jax.random.normal(ks[i], shape, f32) * scale

    def gain(i, shape):
        return 1.0 + 0.05 * jax.random.normal(ks[i], shape, f32)

    D = D_MODEL
    return {
        'x_prompt': nrm(0, (BATCH, SEQ, D)),
        'x_sample': nrm(1, (DEC_BATCH, DEC_SEQ, D)),
        'cache_mla_ckv': nrm(2, (DEC_BATCH, DEPTH, PAST_LEN, MLA_KV_RANK)),
        'cache_mla_krope': nrm(3, (DEC_BATCH, DEPTH, PAST_LEN, MLA_ROPE)),
        'cache_swa_k': nrm(4, (DEC_BATCH, DEPTH, PAST_LEN, SWA_KV_HEADS, HEAD_DIM)),
        'cache_swa_v': nrm(5, (DEC_BATCH, DEPTH, PAST_LEN, SWA_KV_HEADS, HEAD_DIM)),
        'cache_na_k': nrm(6, (DEC_BATCH, DEPTH, PAST_LEN, NA_HEADS, HEAD_DIM)),
        'cache_na_v': nrm(7, (DEC_BATCH, DEPTH, PAST_LEN, NA_HEADS, HEAD_DIM)),
        'cache_diff_k': nrm(8, (DEC_BATCH, DEPTH, PAST_LEN, DIFF_HEADS, 2 * DIFF_QK)),
        'cache_diff_v': nrm(9, (DEC_BATCH, DEPTH, PAST_LEN, DIFF_HEADS, DIFF_V)),
        'c': nrm(10, (DEC_BATCH, D)),
        'c_ctx': nrm(11, (D,)),
        'w_ada': nrm(12, (DEPTH, D, 6 * D), 0.5 * D ** -0.5),
        'b_ada': nrm(13, (DEPTH, 6 * D), 0.01),
        'mix_pre_g': gain(14, (DEPTH, D)),
        'mix_post_g': gain(15, (DEPTH, D)),
        'ffn_pre_g': gain(16, (DEPTH, D)),
        'ffn_post_g': gain(17, (DEPTH, D)),
        'w_in': nrm(18, (DEPTH, D, IN_WIDTH), D ** -0.5),
        'mla_q_norm_g': gain(19, (DEPTH, MLA_Q_RANK)),
        'mla_kv_norm_g': gain(20, (DEPTH, MLA_KV_RANK)),
        'w_mla_uq': nrm(21, (DEPTH, MLA_Q_RANK, MLA_HEADS * (MLA_NOPE + MLA_ROPE)), MLA_Q_RANK ** -0.5),
        'w_mla_ukv': nrm(22, (DEPTH, MLA_KV_RANK, MLA_HEADS * (MLA_NOPE + MLA_V)), MLA_KV_RANK ** -0.5),
        'swa_sink': nrm(23, (DEPTH, SWA_HEADS)),
        'na_rpb': nrm(24, (DEPTH, NA_HEADS, 2 * NA_ROWS - 1, 2 * NA_COLS - 1), 0.5),
        'diff_lq1': nrm(25, (DEPTH, DIFF_QK), 0.1),
        'diff_lk1': nrm(26, (DEPTH, DIFF_QK), 0.1),
        'diff_lq2': nrm(27, (DEPTH, DIFF_QK), 0.1),
        'diff_lk2': nrm(28, (DEPTH, DIFF_QK), 0.1),
        'diff_subln_g': gain(29, (DEPTH, DIFF_V)),
        'w_mix_gate': nrm(30, (DEPTH, N_BRANCH, D, D), D ** -0.5),
        'w_branch': nrm(31, (DEPTH, N_BRANCH, BRANCH_W, D), BRANCH_W ** -0.5),
        'w_out': nrm(32, (DEPTH, D, D), D ** -0.5),
        'w_ffn_gate': nrm(33, (DEPTH, D, FFN_HIDDEN), D ** -0.5),
        'w_ffn_up': nrm(34, (DEPTH, D, FFN_HIDDEN), D ** -0.5),
        'w_ffn_down': nrm(35, (DEPTH, FFN_HIDDEN, D), FFN_HIDDEN ** -0.5),
    }


def reference(x_prompt, x_sample, cache_mla_ckv, cache_mla_krope, cache_swa_k, cache_swa_v,
              cache_na_k, cache_na_v, cache_diff_k, cache_diff_v, c, c_ctx,
              w_ada, b_ada, mix_pre_g, mix_post_g, ffn_pre_g, ffn_post_g, w_in,
              mla_q_norm_g, mla_kv_norm_g, w_mla_uq, w_mla_ukv, swa_sink, na_rpb,
              diff_lq1, diff_lk1, diff_lq2, diff_lk2, diff_subln_g,
              w_mix_gate, w_branch, w_out, w_ffn_gate, w_ffn_up, w_ffn_down):
    def layer_params(l):
        return {
            'w_ada': w_ada[l], 'b_ada': b_ada[l],
            'mix_pre_g': mix_pre_g[l], 'mix_post_g': mix_post_g[l],
            'ffn_pre_g': ffn_pre_g[l], 'ffn_post_g': ffn_post_g[l],
            'w_in': w_in[l], 'mla_q_norm_g': mla_q_norm_g[l], 'mla_kv_norm_g': mla_kv_norm_g[l],
            'w_mla_uq': w_mla_uq[l], 'w_mla_ukv': w_mla_ukv[l],
            'swa_sink': swa_sink[l], 'na_rpb': na_rpb[l],
            'diff_lq1': diff_lq1[l], 'diff_lk1': diff_lk1[l],
            'diff_lq2': diff_lq2[l], 'diff_lk2': diff_lk2[l], 'diff_subln_g': diff_subln_g[l],
            'w_mix_gate': w_mix_gate[l], 'w_branch': w_branch[l], 'w_out': w_out[l],
            'w_ffn_gate': w_ffn_gate[l], 'w_ffn_up': w_ffn_up[l], 'w_ffn_down': w_ffn_down[l],
        }

    y = x_prompt
    c_ctx_row = c_ctx[None, :]
    per_layer = [[] for _ in range(8)]
    for l in range(DEPTH):
        y, own = trunk_layer(y, c_ctx_row, layer_params(l), lambda_init(l), None)
        for j in range(8):
            per_layer[j].append(own[j])
    new_mla_ckv = jnp.stack(per_layer[0], axis=1)
    new_mla_krope = jnp.stack(per_layer[1], axis=1)
    new_swa_k = jnp.stack(per_layer[2], axis=1)
    new_swa_v = jnp.stack(per_layer[3], axis=1)
    new_na_k = jnp.stack(per_layer[4], axis=1)
    new_na_v = jnp.stack(per_layer[5], axis=1)
    new_diff_k = jnp.stack(per_layer[6], axis=1)
    new_diff_v = jnp.stack(per_layer[7], axis=1)

    z = x_sample
    for l in range(DEPTH):
        cache_l = (cache_mla_ckv[:, l], cache_mla_krope[:, l], cache_swa_k[:, l], cache_swa_v[:, l],
                   cache_na_k[:, l], cache_na_v[:, l], cache_diff_k[:, l], cache_diff_v[:, l])
        z, _ = trunk_layer(z, c, layer_params(l), lambda_init(l), cache_l)

    return (y, z, new_mla_ckv, new_mla_krope, new_swa_k, new_swa_v, new_na_k, new_na_v, new_diff_k, new_diff_v)
```

```python
import contextlib
import math
import numpy as np
import concourse.bass as bass
import concourse.mybir as mybir
from concourse.bass_utils import run_bass_kernel_spmd

F32 = mybir.dt.float32
BF16 = mybir.dt.bfloat16
U8 = mybir.dt.uint8
AF = mybir.ActivationFunctionType
ALU = mybir.AluOpType

PE, ACT, DVE, POOL, SP = "pe", "act", "dve", "pool", "sp"
ENGS = (PE, ACT, DVE, POOL, SP)
SEM_CAP = 16000
N_DMA_SEMS = 24

D_MODEL = 2048
NCH = 16
T = 1024
TT = 512
DEPTH = 2
FFN_H = 5632
NFC = 44
EPS = 1e-6
IN_WIDTH = 4672
SLOT_ELEMS = 1024
import os
DBG_NOKOUT = bool(os.environ.get('DBG_NOKOUT'))
NSLOT = 6


def lambda_init(l):
    return 0.8 - 0.6 * math.exp(-0.3 * l)


class Op:
    __slots__ = ("eng", "emit", "deps", "signal", "ev", "is_dma", "dsem", "dval")

    def __init__(self, eng, emit, is_dma=False):
        self.eng = eng
        self.emit = emit
        self.deps = []
        self.signal = False
        self.ev = None
        self.is_dma = is_dma
        self.dsem = None
        self.dval = None


class Sched:
    def __init__(self):
        self.ops = {e: [] for e in ENGS}
        self.lastw = {}
        self.readers = {}
        self.dma_rr = {POOL: 0, SP: 0}
        self.dma_last = [None] * N_DMA_SEMS
        self.dma_cnt = [0] * N_DMA_SEMS
        self.fence = []

    def barrier(self):
        self.fence = [self.ops[e][-1] for e in (PE, ACT, DVE) if self.ops[e]]
        self.fence += [op for op in self.dma_last if op is not None and op.eng == SP]

    def add(self, eng, emit, reads=(), writes=(), dma=False, nofence=False):
        op = Op(eng, emit, dma)
        deps = [] if nofence else list(self.fence)
        for t in reads:
            w = self.lastw.get(t)
            if w is not None:
                deps.append(w)
            if type(t) is tuple and t[0] == "ps":
                deps.extend(r for r in self.readers.get(t, ()) if r.eng != eng)
        for t in writes:
            w = self.lastw.get(t)
            if w is not None:
                deps.append(w)
            deps.extend(self.readers.get(t, ()))
        if dma:
            half = N_DMA_SEMS // 2
            r = self.dma_rr[eng]
            self.dma_rr[eng] = (r + 1) % half
            s = r + (half if eng == SP else 0)
            prev = self.dma_last[s]
            if prev is not None:
                deps.append(prev)
            self.dma_last[s] = op
            self.dma_cnt[s] += 1
            op.dsem = s
            op.dval = 16 * self.dma_cnt[s]
        seen = set()
        for d in deps:
            if d is op or id(d) in seen:
                continue
            seen.add(id(d))
            if (not d.is_dma) and d.eng == eng and eng in (PE, SP):
                continue
            op.deps.append(d)
            d.signal = True
        for t in reads:
            self.readers.setdefault(t, []).append(op)
        for t in writes:
            self.lastw[t] = op
            self.readers[t] = []
        self.ops[eng].append(op)
        return op

    def emit_all(self, nc, final_wait_ops=()):
        for d in final_wait_ops:
            d.signal = True
        eng_sems = {}
        for e in ENGS:
            cnt = 0
            k = 0
            for op in self.ops[e]:
                if op.is_dma:
                    op.ev = ("dma", op.dsem, op.dval)
                elif op.signal:
                    if cnt >= SEM_CAP:
                        k += 1
                        cnt = 0
                    cnt += 1
                    op.ev = (e, k, cnt)
            eng_sems[e] = k + 1
        with contextlib.ExitStack() as st:
            sems = {}
            for e in ENGS:
                for k in range(eng_sems[e]):
                    sems[(e, k)] = st.enter_context(nc.semaphore(f"s_{e}_{k}"))
            for s in range(N_DMA_SEMS):
                sems[("dma", s)] = st.enter_context(nc.semaphore(f"s_dma_{s}"))
            block = st.enter_context(nc.Block())
            tail = list(final_wait_ops)

            def run(e, eng):
                waited = {}
                for op in self.ops[e]:
                    for d in op.deps:
                        key = (d.ev[0], d.ev[1])
                        val = d.ev[2]
                        if waited.get(key, 0) < val:
                            eng.wait_ge(sems[key], val)
                            waited[key] = val
                    ins = op.emit(eng)
                    if op.is_dma:
                        ins.then_inc(sems[("dma", op.dsem)], 16)
                    elif op.signal:
                        ins.then_inc(sems[(op.ev[0], op.ev[1])], 1)
                if e == SP:
                    for d in tail:
                        key = (d.ev[0], d.ev[1])
                        if waited.get(key, 0) < d.ev[2]:
                            eng.wait_ge(sems[key], d.ev[2])
                            waited[key] = d.ev[2]

            @block.tensor
            def _(eng):
                run(PE, eng)

            @block.scalar
            def _(eng):
                run(ACT, eng)

            @block.vector
            def _(eng):
                run(DVE, eng)

            @block.gpsimd
            def _(eng):
                run(POOL, eng)

            @block.sync
            def _(eng):
                run(SP, eng)


def I(method, *args, **kwargs):
    return lambda e: getattr(e, method)(*args, **kwargs)


SM_GAIN = 0
SM_BADA = 64
SM_QG = 160
SM_KVG = 163
SM_SUBG = 164
SM_SINK = 165
SM_LAM = 169
NSMALL = 169 + 256

_o = np.cumsum([0, 384, 128, 64, 512, 256, 256, 512, 512, 512, 512, 512, 512])
COL = dict(qdn=_o[0], ckv=_o[1], kro=_o[2], qb=_o[3], kb=_o[4], vb=_o[5], qc=_o[6], kc=_o[7],
           vc=_o[8], qd=_o[9], kd=_o[10], vd=_o[11])


def win_blocks():
    bl = []
    for i in range(3):
        bl.append(("qdn", i, COL["qdn"] + 128 * i, 128))
    bl.append(("ckv", 0, COL["ckv"], 128))
    bl.append(("kro", 0, COL["kro"], 64))
    for nm, n in (("qb", 4), ("kb", 2), ("vb", 2), ("qc", 4), ("kc", 4), ("vc", 4),
                  ("qd", 4), ("kd", 4), ("vd", 4)):
        for i in range(n):
            bl.append((nm, i, COL[nm] + 128 * i, 128))
    return bl


WIN_BLOCKS = win_blocks()
WIN_IDX = {(nm, i): k for k, (nm, i, _, _) in enumerate(WIN_BLOCKS)}


def build_program(layers=(0, 1), phases=("P", "S"), stop=None):
    nc = bass.Bass("TRN2", target_bir_lowering=False)
    S = Sched()
    L = DEPTH

    def din(name, shape):
        return nc.dram_tensor(name, list(shape), F32, kind="ExternalInput").ap()

    def dout(name, shape):
        return nc.dram_tensor(name, list(shape), F32, kind="ExternalOutput").ap()

    xin = {"P": din("xP", [D_MODEL, T]), "S": din("xS", [D_MODEL, T])}
    cvec = din("cvec", [128, 32])
    small = din("small", [L, 128, NSMALL])
    cperm = din("cperm", [2, 128, 128])
    ceye = din("ceye", [2, 2])
    ropetab = din("ropetab", [4, 128, T])
    cmask = din("cmask", [3, 128, 128])
    nabias = din("nabias", [L, 128, 4 * 16 * 64])
    w_ada = din("w_ada_t", [L, 24 * 8, 128, 1024])
    w_in = din("w_in_t", [L, len(WIN_BLOCKS) * 2, 128, 1024])
    w_uq = din("w_uq_t", [L, 128, 8 * 384])
    w_ukv = din("w_ukv_t", [L, 128, 1024])
    w_gate = din("w_gate_t", [L, 16 * 4 * 2, 128, 1024])
    w_br = din("w_br_t", [L, 16 * 2, 128, 1024])
    w_out = din("w_out_t", [L, 16 * 2, 128, 1024])
    w_gu = din("w_gu_t", [L, NFC * 2 * 2, 128, 1024])
    w_dn = din("w_dn_t", [L, 16 * 6, 128, 1024])
    kc_mla = din("kc_mla", [L, 128, 512])
    kc_kro = din("kc_kro", [L, 64, 512])
    kc_b = din("kc_b", [L, 256, 512])
    kc_c = din("kc_c", [L, 512, 512])
    kc_d = din("kc_d", [L, 512, 512])
    vc_b = din("vc_b", [L, 512, 256])
    vc_c = din("vc_c", [L, 512, 512])
    vc_d = din("vc_d", [L, 512, 512])
    yout = {"P": dout("yP", [D_MODEL, T]), "S": dout("yS", [D_MODEL, T])}
    o_ckv = dout("o_ckv", [L, 128, T])
    o_kro = dout("o_kro", [L, 64, T])
    o_kb = dout("o_kb", [L, 256, T])
    o_vb = dout("o_vb", [L, T, 256])
    o_kc = dout("o_kc", [L, 512, T])
    o_vc = dout("o_vc", [L, T, 512])
    o_kd = dout("o_kd", [L, 512, T])
    o_vd = dout("o_vd", [L, T, 512])
    out_ops = []

    OFF_X = 0
    OFF_H = 65536
    OFF_BIG = 98304
    BIG_SZ = 90112
    OFF_W = OFF_BIG + BIG_SZ
    OFF_SCR = OFF_W + NSLOT * 2048
    SCR_SZ = 8192
    OFF_CST = OFF_SCR + SCR_SZ
    CST_SZ = 3936
    TOTAL = OFF_CST + CST_SZ
    arena = nc.alloc_sbuf_tensor("arena", [128, TOTAL], U8)

    def view(off, nbytes, dt):
        return arena[:, off:off + nbytes].bitcast(dt)

    X = view(OFF_X, 65536, F32).rearrange("p (c t) -> p c t", c=NCH)
    H = view(OFF_H, 32768, BF16).rearrange("p (c t) -> p c t", c=NCH)
    MO = view(OFF_H, 32768, F32).rearrange("p (c t) -> p c t", c=NCH)
    ACTB = view(OFF_BIG, NFC * 2048, BF16).rearrange("p (c t) -> p c t", c=NFC)
    OB = view(OFF_BIG, 32768, BF16).rearrange("p (c t) -> p c t", c=16)
    MIXB = view(OFF_BIG + 32768, 32768, BF16).rearrange("p (c t) -> p c t", c=16)
    OFF_ROPE = OFF_BIG + 32768
    ROPE_C = view(OFF_ROPE, 4096, F32)
    ROPE_S = view(OFF_ROPE + 4096, 4096, F32)
    ETAB = view(OFF_ROPE, 8192, BF16).rearrange("p (h i c) -> p h i c", h=4, i=16)
    OFF_QKV = OFF_BIG + 40960
    QKV_SZ = 45056
    OFF_PST = OFF_QKV + QKV_SZ
    PST = [view(OFF_PST + 1024 * i, 1024, BF16) for i in range(4)]
    OSTG = [view(OFF_QKV + 36864 + 2048 * i, 2048, F32) for i in range(4)]
    WS = [view(OFF_W + 2048 * i, 2048, BF16) for i in range(NSLOT)]
    T0 = view(OFF_SCR, 2048, F32)
    T1 = view(OFF_SCR + 2048, 2048, F32)
    RSTD = view(OFF_SCR + 4096, 2048, F32)
    SQ = [view(OFF_SCR + 6144 + 1024 * i, 1024, BF16) for i in range(2)]
    co = [OFF_CST]

    def calloc(nbytes, dt):
        v = view(co[0], nbytes, dt)
        co[0] += nbytes
        assert co[0] <= TOTAL
        return v

    ONES = calloc(256, BF16)
    PERM = [calloc(256, BF16), calloc(256, BF16)]
    EYE2 = calloc(8, F32)
    CEPS = calloc(8, F32)
    SC = calloc(64, BF16).rearrange("p (c j) -> p c j", j=2)
    MODT = [calloc(768, F32).rearrange("p (c j) -> p c j", j=2) for _ in range(L)]
    SMALL = calloc(SM_LAM * 4, F32)
    VECF = calloc(4 * 64, F32)
    VEC = VECF.rearrange("p (v c) -> p v c", v=4)
    CV = VECF[:, 0:32]
    MASKS = calloc(512, BF16).rearrange("p (m c) -> p m c", m=2)
    MISC = calloc(64, F32)
    PS = [nc.alloc_psum_tensor(f"ps{i}", [128, 512], F32) for i in range(8)]

    st = {"slot": 0, "bank": 0, "pst": 0, "tq": 0, "sq": 0, "sring": 0, "ostg": 0}

    def nostg():
        k = st["ostg"] % 4
        st["ostg"] += 1
        return OSTG[k], ("ostg", k)

    def wload(src_ap, nelem=SLOT_ELEMS):
        k = st["slot"]
        st["slot"] = (k + 1) % NSLOT
        dst = WS[k][:, 0:nelem]
        src = src_ap if nelem == SLOT_ELEMS else src_ap[:, 0:nelem]
        S.add(POOL, I("dma_start", out=dst, in_=src), writes=[("w", k)], dma=True, nofence=True)
        return WS[k], ("w", k)

    def nbank(ring=(0, 1, 2, 3)):
        b = ring[st["bank"] % len(ring)]
        st["bank"] += 1
        return b

    def npst():
        k = st["pst"] % 4
        st["pst"] += 1
        return k

    def nsq():
        k = st["sq"] % 2
        st["sq"] += 1
        return k

    def mm_group(bank, pr, cols, pairs, reads, extra_writes=(), first=True, last=True):
        n = len(pairs)

        def emit(e):
            ins = None
            for i, (lt, rh) in enumerate(pairs):
                ins = e.matmul(PS[bank][pr[0]:pr[1], cols[0]:cols[1]], lt, rh,
                               start=(first and i == 0), stop=(last and i == n - 1))
            return ins
        return S.add(PE, emit, reads=list(reads), writes=[("ps", bank)] + list(extra_writes))

    def sp_load(dst, src, tok):
        return S.add(SP, I("dma_start", out=dst, in_=src), writes=[tok], dma=True)

    def pool_load(dst, src, tok):
        return S.add(POOL, I("dma_start", out=dst, in_=src), writes=[tok], dma=True)

    def store(dst, src, tok):
        op = S.add(SP, I("dma_start", out=dst, in_=src), reads=[tok], dma=True)
        out_ops.append(op)
        return op

    S.add(DVE, I("memset", ONES, 1.0), writes=["ones"])
    S.add(DVE, I("memset", CEPS, EPS), writes=["ceps"])
    pool_load(PERM[0], cperm[0], "perm")
    pool_load(PERM[1], cperm[1], "perm")
    sp_load(EYE2[0:2, :], ceye, "eye2")
    sp_load(CV, cvec, "vec")
    S.add(ACT, I("activation", out=SC.rearrange("p c j -> p (c j)"), in_=CV, func=AF.Silu),
          reads=["vec"], writes=["sc"])
    pool_load(MASKS[:, 0, :], cmask[0], "masks")
    pool_load(MASKS[:, 1, :], cmask[1], "masks")

    def load_small(l):
        sp_load(SMALL, small[l, :, 0:SM_LAM], "small")

    def compute_mod(l, jbs=range(24), small_tile=None, small_tok="small", ring=(0, 1, 2, 3), rowtmp=None, rowtoks=("t0",)):
        sm_ = SMALL[:, SM_BADA:SM_BADA + 96] if small_tile is None else small_tile
        rt_ = T0 if rowtmp is None else rowtmp
        for jb in jbs:
            bank = nbank(ring)
            for g in range(8):
                tl = wload(w_ada[l, jb * 8 + g])
                pairs = [(SC[:, 2 * g + kk, :], tl[0][:, kk * 512:(kk + 1) * 512]) for kk in range(2)]
                mm_group(bank, (0, 2), (0, 512), pairs, ["sc", tl[1]], first=(g == 0), last=(g == 7))
            S.add(ACT, I("activation", out=rt_[0:2, :], in_=PS[bank][0:2, :], func=AF.Identity),
                  reads=[("ps", bank)], writes=list(rowtoks))
            b2 = nbank(ring)

            def emit(e, b2=b2):
                ins = None
                for c in range(4):
                    ins = e.matmul(PS[b2][:, 2 * c:2 * c + 2], rt_[0:2, c * 128:(c + 1) * 128], EYE2[0:2, :],
                                   start=True, stop=True)
                return ins
            S.add(PE, emit, reads=list(rowtoks) + ["eye2"], writes=[("ps", b2)])
            S.add(DVE, I("tensor_tensor",
                out=MODT[l][:, jb * 4:(jb + 1) * 4, :],
                in0=PS[b2][:, 0:8].rearrange("p (c j) -> p c j", j=2),
                in1=sm_[:, jb * 4:(jb + 1) * 4].unsqueeze(2).broadcast_to([128, 4, 2]),
                op=ALU.add), reads=[("ps", b2), small_tok], writes=[("modt", l)])

    def layer_vectors(l, j):
        m = MODT[l]
        gsl = lambda k: SMALL[:, SM_GAIN + 16 * k:SM_GAIN + 16 * (k + 1)]
        S.add(DVE, I("scalar_tensor_tensor", out=VEC[:, 0, :], in0=m[:, 16:32, j], scalar=1.0, in1=gsl(0),
                                                    op0=ALU.add, op1=ALU.mult),
              reads=[("modt", l), "small"], writes=["vec"])
        S.add(DVE, I("tensor_tensor", out=VEC[:, 1, :], in0=m[:, 32:48, j], in1=gsl(1), op=ALU.mult),
              reads=[("modt", l), "small"], writes=["vec"])
        S.add(DVE, I("scalar_tensor_tensor", out=VEC[:, 2, :], in0=m[:, 64:80, j], scalar=1.0, in1=gsl(2),
                                                    op0=ALU.add, op1=ALU.mult),
              reads=[("modt", l), "small"], writes=["vec"])
        S.add(DVE, I("tensor_tensor", out=VEC[:, 3, :], in0=m[:, 80:96, j], in1=gsl(3), op=ALU.mult),
              reads=[("modt", l), "small"], writes=["vec"])

    def rstd_from_bank(bank, pr, n, inv_n, dst, dst_tok):
        p0, p1 = pr
        S.add(ACT, I("activation", out=T1[p0:p1, 0:n], in_=PS[bank][p0:p1, 0:n], func=AF.Ln,
                     bias=CEPS[p0:p1, 0:1], scale=inv_n),
              reads=[("ps", bank), "ceps"], writes=["t1"])
        S.add(ACT, I("activation", out=dst[p0:p1, 0:n], in_=T1[p0:p1, 0:n], func=AF.Exp, scale=-0.5),
              reads=["t1"], writes=[dst_tok])

    def norm_mod(l, j, avec, shift_base):
        for t in range(2):
            ts = slice(t * TT, (t + 1) * TT)
            bank = nbank()
            for c in range(NCH):
                k = nsq()
                S.add(ACT, I("activation", out=SQ[k], in_=X[:, c, ts], func=AF.Square),
                      reads=[("x", c, t)], writes=[("sq", k)])
                S.add(PE, I("matmul", PS[bank][:, :], ONES, SQ[k], start=(c == 0), stop=(c == NCH - 1)),
                      reads=[("sq", k), "ones"], writes=[("ps", bank)])
            rstd_from_bank(bank, (0, 128), TT, 1.0 / D_MODEL, RSTD, "rstd")
            for c in range(NCH):
                tb, tk = (T0, "t0") if c % 2 == 0 else (T1, "t1")
                S.add(DVE, I("scalar_tensor_tensor",
                    out=tb, in0=X[:, c, ts], scalar=VEC[:, avec, c:c + 1], in1=RSTD, op0=ALU.mult, op1=ALU.mult),
                    reads=[("x", c, t), "vec", "rstd"], writes=[tk])
                S.add(ACT, I("activation",
                    out=H[:, c, ts], in_=tb, func=AF.Identity, bias=MODT[l][:, shift_base + c, j:j + 1], scale=1.0),
                    reads=[tk, ("modt", l)], writes=[("h", c, t)])

    def hreads(t):
        return [("h", c, t) for c in range(NCH)]

    def proj_fm(l, blk, ncols=128):
        k = WIN_IDX[blk]
        tiles = [wload(w_in[l, 2 * k + g]) for g in range(2)]
        banks = []
        for t in range(2):
            bank = nbank()
            pairs = [(tiles[kc // 8][0][:, (kc % 8) * 128:(kc % 8) * 128 + ncols], H[:, kc, t * TT:(t + 1) * TT])
                     for kc in range(NCH)]
            mm_group(bank, (0, ncols), (0, TT), pairs, hreads(t) + [tl[1] for tl in tiles])
            banks.append(bank)
        return banks

    def proj_tm(l, blk, vdst, vtok, wcol, outd, ph):
        k = WIN_IDX[blk]
        tiles = [wload(w_in[l, 2 * k + g]) for g in range(2)]
        koff = 4 if ph == "S" else 0
        for half in range(2):
            bank = nbank()
            for q in range(4):
                tt = half * 4 + q
                t = tt // 4
                pairs = [(H[:, kc, tt * 128:(tt + 1) * 128], tiles[kc // 8][0][:, (kc % 8) * 128:(kc % 8) * 128 + 128])
                         for kc in range(NCH)]
                mm_group(bank, (0, 128), (q * 128, (q + 1) * 128), pairs, hreads(t) + [tl[1] for tl in tiles])
            S.add(ACT, I("activation",
                out=vdst[:, koff + half * 4:koff + half * 4 + 4, wcol:wcol + 128],
                in_=PS[bank][:, :].rearrange("p (q c) -> p q c", q=4), func=AF.Identity),
                reads=[("ps", bank)], writes=[vtok])
            if ph == "P":
                tb, tk = nostg()
                S.add(ACT, I("activation", out=tb, in_=PS[bank][:, :], func=AF.Identity),
                      reads=[("ps", bank)], writes=[tk])
                store(outd[l, half * 512:(half + 1) * 512, wcol:wcol + 128].rearrange("(q p) c -> p q c", p=128),
                      tb.rearrange("p (q c) -> p q c", q=4), tk)

    def evac_fm(bank, t, dst, dtok, ph, rope=None, np_=128, outd=None):
        ts = slice(t * TT, (t + 1) * TT)
        if rope is None:
            S.add(ACT, I("activation", out=dst, in_=PS[bank][0:np_, :], func=AF.Identity),
                  reads=[("ps", bank)], writes=[dtok])
        else:
            perm = PERM[rope]
            k = npst()
            S.add(ACT, I("activation", out=PST[k][0:np_, :], in_=PS[bank][0:np_, :], func=AF.Identity),
                  reads=[("ps", bank)], writes=[("pst", k)])
            b2 = nbank()
            mm_group(b2, (0, np_), (0, TT), [(perm[0:np_, 0:np_], PST[k][0:np_, :])], [("pst", k), "perm"])
            S.add(DVE, I("tensor_tensor", out=T0[0:np_, :], in0=PS[bank][0:np_, :], in1=ROPE_C[0:np_, ts], op=ALU.mult),
                  reads=[("ps", bank), "rope"], writes=["t0"])
            S.add(DVE, I("tensor_tensor", out=T1[0:np_, :], in0=PS[b2][0:np_, :], in1=ROPE_S[0:np_, ts], op=ALU.mult),
                  reads=[("ps", b2), "rope"], writes=["t1"])
            S.add(DVE, I("tensor_tensor", out=dst, in0=T0[0:np_, :], in1=T1[0:np_, :], op=ALU.add),
                  reads=["t0", "t1"], writes=[dtok])
        if outd is not None:
            ob_, ok_ = nostg()
            S.add(ACT, I("activation", out=ob_[0:np_, :], in_=PS[bank][0:np_, :], func=AF.Identity), reads=[("ps", bank)], writes=[ok_])
            store(outd[:, ts], ob_[0:np_, :], ok_)

    def load_rope(which):
        sp_load(ROPE_C, ropetab[2 * which], "rope")
        sp_load(ROPE_S, ropetab[2 * which + 1], "rope")

    acc_sel = [0]
    pend = []
    LOOKAHEAD = 2

    def attn_flush(keep=0):
        while len(pend) > keep:
            pend.pop(0)()

    def attn_unit(scale, nq, kblocks, extra_reads, banks=None, finish=None):
        if banks is None:
            ob, db = (4, 5) if acc_sel[0] % 2 == 0 else (6, 7)
            acc_sel[0] += 1
        else:
            ob, db = banks
        nb = len(kblocks)
        for i, kb in enumerate(kblocks):
            sb = st["sring"] % 3
            st["sring"] += 1
            mm_group(sb, (0, 128), (0, nq), kb["parts"], extra_reads)

            def proc(i=i, kb=kb, sb=sb):
                p0, p1 = kb["pr"]
                k = npst()
                S.add(ACT, I("activation", out=PST[k][p0:p1, 0:nq], in_=PS[sb][p0:p1, 0:nq], func=AF.Exp, scale=scale),
                      reads=[("ps", sb)], writes=[("pst", k)])
                if kb.get("mask") is not None:
                    S.add(DVE, I("tensor_tensor", out=PST[k][p0:p1, 0:nq], in0=PST[k][p0:p1, 0:nq], in1=kb["mask"], op=ALU.mult),
                          reads=[("pst", k)] + list(kb.get("mreads", [])), writes=[("pst", k)])

                def emit(e):
                    e.matmul(PS[ob][:, 0:nq], kb["v"], PST[k][p0:p1, 0:nq], start=(i == 0), stop=(i == nb - 1))
                    return e.matmul(PS[db][:, 0:nq], ONES[p0:p1, :], PST[k][p0:p1, 0:nq], start=(i == 0), stop=(i == nb - 1))
                S.add(PE, emit, reads=[("pst", k), "ones"] + list(extra_reads), writes=[("ps", ob), ("ps", db)])
                if i == nb - 1 and finish is not None:
                    finish(ob, db)
            pend.append(proc)
            attn_flush(LOOKAHEAD)
        return ob, db

    def attn_finish(ob, db, nq, dst, dtok, sink_col=None):
        if sink_col is not None:
            S.add(ACT, I("activation", out=T1[:, 0:nq], in_=PS[db][:, 0:nq], func=AF.Ln, bias=sink_col, scale=1.0),
                  reads=[("ps", db), "misc"], writes=["t1"])
        else:
            S.add(ACT, I("activation", out=T1[:, 0:nq], in_=PS[db][:, 0:nq], func=AF.Ln),
                  reads=[("ps", db)], writes=["t1"])
        S.add(ACT, I("activation", out=T0[:, 0:nq], in_=T1[:, 0:nq], func=AF.Exp, scale=-1.0), reads=["t1"], writes=["t0"])
        S.add(DVE, I("tensor_tensor", out=dst, in0=PS[ob][:, 0:nq], in1=T0[:, 0:nq], op=ALU.mult),
              reads=[("ps", ob), "t0"], writes=[dtok])

    def qkv_views(ph):
        nk = T + (512 if ph == "S" else 0)
        nkt = nk // 128
        return nk, nkt

    def mixer_A(l, ph):
        nk, nkt = qkv_views(ph)
        koff = nk - T
        o = [OFF_QKV]

        def al(nbytes, dt):
            v = view(o[0], nbytes, dt)
            o[0] += nbytes
            assert o[0] <= OFF_QKV + QKV_SZ
            return v
        UQ = al(6144, BF16)
        UKV = al(2048, BF16)
        CQ = al(6144, BF16).rearrange("p (c t) -> p c t", c=3)
        CKVT = al(nk * 2, BF16)
        KRT = al(nk * 2, BF16)
        QN = [al(2048, BF16) for _ in range(2)]
        QR = [al(2048, BF16) for _ in range(2)]
        KNT = [al(nk * 2, BF16) for _ in range(2)]
        VH = [al(nkt * 256, BF16).rearrange("p (k d) -> p k d", d=128) for _ in range(2)]
        pool_load(UQ, w_uq[l], "uq")
        pool_load(UKV, w_ukv[l], "ukv")
        if ph == "S":
            load_rope(1)
            pool_load(CKVT[:, 0:512], kc_mla[l], "ckvt")
            pool_load(KRT[0:64, 0:512], kc_kro[l], "krt")
        st["bank"] = 0
        qb = {}
        for c in range(3):
            k = WIN_IDX[("qdn", c)]
            tiles = [wload(w_in[l, 2 * k + g]) for g in range(2)]
            for t in range(2):
                bank = c * 2 + t
                pairs = [(tiles[kc // 8][0][:, (kc % 8) * 128:(kc % 8) * 128 + 128], H[:, kc, t * TT:(t + 1) * TT])
                         for kc in range(NCH)]
                mm_group(bank, (0, 128), (0, TT), pairs, hreads(t) + [tl[1] for tl in tiles])
                qb[(c, t)] = bank
        for t in range(2):
            sbk = 6 + t
            for c in range(3):
                k = nsq()
                S.add(ACT, I("activation", out=SQ[k], in_=PS[qb[(c, t)]][:, :], func=AF.Square),
                      reads=[("ps", qb[(c, t)])], writes=[("sq", k)])
                S.add(PE, I("matmul", PS[sbk][:, :], ONES, SQ[k], start=(c == 0), stop=(c == 2)),
                      reads=[("sq", k), "ones"], writes=[("ps", sbk)])
            rstd_from_bank(sbk, (0, 128), TT, 1.0 / 384, RSTD, "rstd")
            for c in range(3):
                S.add(DVE, I("scalar_tensor_tensor",
                    out=CQ[:, c, t * TT:(t + 1) * TT], in0=PS[qb[(c, t)]][:, :], scalar=SMALL[:, SM_QG + c:SM_QG + c + 1],
                    in1=RSTD, op0=ALU.mult, op1=ALU.mult),
                    reads=[("ps", qb[(c, t)]), "rstd", "small"], writes=[("cq", t)])
        cb = proj_fm(l, ("ckv", 0))
        for t in range(2):
            k = nsq()
            S.add(ACT, I("activation", out=SQ[k], in_=PS[cb[t]][:, :], func=AF.Square),
                  reads=[("ps", cb[t])], writes=[("sq", k)])
            sbk = nbank((4, 5, 6, 7))
            mm_group(sbk, (0, 128), (0, TT), [(ONES, SQ[k])], [("sq", k), "ones"])
            rstd_from_bank(sbk, (0, 128), TT, 1.0 / 128, RSTD, "rstd")
            S.add(DVE, I("scalar_tensor_tensor",
                out=T0, in0=PS[cb[t]][:, :], scalar=SMALL[:, SM_KVG:SM_KVG + 1], in1=RSTD, op0=ALU.mult, op1=ALU.mult),
                reads=[("ps", cb[t]), "rstd", "small"], writes=["t0"])
            S.add(ACT, I("activation", out=CKVT[:, koff + t * TT:koff + (t + 1) * TT], in_=T0, func=AF.Identity),
                  reads=["t0"], writes=["ckvt"])
            if ph == "P":
                store(o_ckv[l, :, t * TT:(t + 1) * TT], T0, "t0")
        kb_ = proj_fm(l, ("kro", 0), ncols=64)
        for t in range(2):
            evac_fm(kb_[t], t, KRT[0:64, koff + t * TT:koff + (t + 1) * TT], "krt", ph,
                    rope=(1 if ph == "S" else None), np_=64, outd=(o_kro[l] if ph == "P" else None))
        sc_a = (128 + 64) ** -0.5
        for h in range(4):
            s2 = h % 2
            attn_flush()
            for t in range(2):
                bank = nbank()
                pairs = [(UQ[:, h * 384 + kc * 128:h * 384 + (kc + 1) * 128], CQ[:, kc, t * TT:(t + 1) * TT]) for kc in range(3)]
                mm_group(bank, (0, 128), (0, TT), pairs, [("cq", t), "uq"])
                S.add(ACT, I("activation", out=QN[s2][:, t * TT:(t + 1) * TT], in_=PS[bank][:, :],
                                                                         func=AF.Identity),
                      reads=[("ps", bank)], writes=[("qn", s2)])
                bank = nbank()
                pairs = [(UQ[:, (4 + h) * 384 + kc * 128:(4 + h) * 384 + kc * 128 + 64], CQ[:, kc, t * TT:(t + 1) * TT])
                         for kc in range(3)]
                mm_group(bank, (0, 64), (0, TT), pairs, [("cq", t), "uq"])
                evac_fm(bank, t, QR[s2][0:64, t * TT:(t + 1) * TT], ("qr", s2), ph, rope=(1 if ph == "S" else None), np_=64)
            for kt in range(nk // TT):
                bank = nbank()
                mm_group(bank, (0, 128), (0, TT), [(UKV[:, h * 128:(h + 1) * 128], CKVT[:, kt * TT:(kt + 1) * TT])],
                         ["ckvt", "ukv"])
                S.add(ACT, I("activation", out=KNT[s2][:, kt * TT:(kt + 1) * TT], in_=PS[bank][:, :],
                                                                           func=AF.Identity),
                      reads=[("ps", bank)], writes=[("knt", s2)])
            for g in range(nkt // 4):
                bank = nbank()
                for q in range(4):
                    kk = g * 4 + q
                    mm_group(bank, (0, 128), (q * 128, (q + 1) * 128),
                             [(CKVT[:, kk * 128:(kk + 1) * 128], UKV[:, 512 + h * 128:512 + (h + 1) * 128])], ["ckvt", "ukv"])
                S.add(ACT, I("activation",
                    out=VH[s2][:, g * 4:(g + 1) * 4, :], in_=PS[bank][:, :].rearrange("p (q c) -> p q c", q=4), func=AF.Identity),
                    reads=[("ps", bank)], writes=[("vh", s2)])
            rd = [("qn", s2), ("qr", s2), ("knt", s2), ("vh", s2), "krt"]
            if ph == "P":
                for s in range(4):
                    q0 = s * 256
                    kbs = []
                    for j in range(2):
                        kc0 = q0 + j * 128
                        kbs.append(dict(parts=[(KNT[s2][:, kc0:kc0 + 128], QN[s2][:, q0:q0 + 256]),
                                               (KRT[0:64, kc0:kc0 + 128], QR[s2][0:64, q0:q0 + 256])],
                                        v=VH[s2][:, 2 * s + j, :], pr=(0, 128)))
                    attn_unit(sc_a, 256, kbs, rd,
                              finish=(lambda ob, db, a_=(256, OB[:, 0 + h, q0:q0 + 256], ("ob", 0 + h)): attn_finish(ob, db, *a_)))
            else:
                for t in range(2):
                    q0 = t * TT
                    kbs = []
                    for j in range(nkt):
                        kc0 = j * 128
                        kbs.append(dict(parts=[(KNT[s2][:, kc0:kc0 + 128], QN[s2][:, q0:q0 + TT]),
                                               (KRT[0:64, kc0:kc0 + 128], QR[s2][0:64, q0:q0 + TT])],
                                        v=VH[s2][:, j, :], pr=(0, 128)))
                    attn_unit(sc_a, TT, kbs, rd,
                              finish=(lambda ob, db, a_=(TT, OB[:, 0 + h, q0:q0 + TT], ("ob", 0 + h)): attn_finish(ob, db, *a_)))

    def proj_qk(l, ph, nm, nblk, dst_fn, tokname, rope, outd):
        for i in range(nblk):
            banks = proj_fm(l, (nm, i))
            for t in range(2):
                evac_fm(banks[t], t, dst_fn(i, t), (tokname, i), ph, rope=rope,
                        outd=(outd[l, i * 128:(i + 1) * 128, :] if (outd is not None and ph == "P" and not DBG_NOKOUT) else None))

    def mixer_BCD(l, ph, which):
        nk, nkt = qkv_views(ph)
        koff = nk - T
        kofft = koff // 128
        o = [OFF_QKV]

        def al(nbytes, dt):
            v = view(o[0], nbytes, dt)
            o[0] += nbytes
            assert o[0] <= OFF_QKV + QKV_SZ
            return v
        nkh = 2 if which == "B" else 4
        vw = nkh * 128
        Q = al(8192, BF16).rearrange("p (h t) -> p h t", h=4)
        KT = al(nkh * nk * 2, BF16).rearrange("p (h t) -> p h t", h=nkh)
        V = al(nkt * vw * 2, BF16).rearrange("p (k c) -> p k c", k=nkt)
        qn, kn, vn = {"B": ("qb", "kb", "vb"), "C": ("qc", "kc", "vc"), "D": ("qd", "kd", "vd")}[which]
        kcs, vcs = {"B": (kc_b, vc_b), "C": (kc_c, vc_c), "D": (kc_d, vc_d)}[which]
        okd, ovd = {"B": (o_kb, o_vb), "C": (o_kc, o_vc), "D": (o_kd, o_vd)}[which]
        mi = {"B": 1, "C": 2, "D": 3}[which]
        rope = None
        if ph == "S":
            if which == "B":
                load_rope(0)
                rope = 0
            elif which == "D":
                load_rope(1)
                rope = 1
            for hh in range(nkh):
                pool_load(KT[:, hh, 0:512], kcs[l, hh * 128:(hh + 1) * 128, :], ("kt", hh))
            pool_load(V[:, 0:4, :], vcs[l].rearrange("(k p) c -> p k c", p=128), "v")
            if which == "C":
                for q in range(8):
                    tb, tk = (T0, "t0") if q % 2 == 0 else (T1, "t1")
                    sp_load(tb, nabias[l, :, q * 512:(q + 1) * 512], tk)
                    hh, i0 = q // 2, (q % 2) * 8
                    S.add(ACT, I("activation",
                        out=ETAB[:, hh, i0:i0 + 8, :], in_=tb.rearrange("p (i c) -> p i c", i=8), func=AF.Exp),
                        reads=[tk], writes=["etab"])
                sp_load(RSTD[:, 0:128], cmask[2], "rstd")
                S.add(DVE, I("tensor_tensor",
                    out=ETAB.rearrange("p h i c -> p (h i) c"), in0=ETAB.rearrange("p h i c -> p (h i) c"),
                    in1=RSTD[:, 0:64].unsqueeze(1).broadcast_to([128, 64, 64]), op=ALU.mult),
                    reads=["etab", "rstd"], writes=["etab"])
        proj_qk(l, ph, qn, 4, lambda i, t: Q[:, i, t * TT:(t + 1) * TT], "q", rope, None)
        stage(which + "_q")
        proj_qk(l, ph, kn, nkh, lambda i, t: KT[:, i, koff + t * TT:koff + (t + 1) * TT], "kt", rope, okd)
        stage(which + "_k")
        for i in range(nkh):
            proj_tm(l, (vn, i), V, "v", i * 128, ovd, ph)
        stage(which + "_v")
        qr = [("q", i) for i in range(4)]
        kr = [("kt", i) for i in range(nkh)]
        rd = qr + kr + ["v"]
        sc = 128 ** -0.5
        if which == "B":
            S.add(ACT, I("activation", out=MISC[:, 0:4], in_=SMALL[:, SM_SINK:SM_SINK + 4], func=AF.Exp),
                  reads=["small"], writes=["misc"])
        if which == "D":
            diff_lambda(l)
        for h in range(4):
            kh = h // 2 if which == "B" else h
            if which in ("B", "C"):
                sink = MISC[:, h:h + 1] if which == "B" else None
                if ph == "P":
                    for s in range(4):
                        q0 = s * 256
                        kbs = [dict(parts=[(KT[:, kh, q0 + j * 128:q0 + (j + 1) * 128], Q[:, h, q0:q0 + 256])],
                                    v=V[:, 2 * s + j, kh * 128:(kh + 1) * 128], pr=(0, 128)) for j in range(2)]
                        attn_unit(sc, 256, kbs, rd,
                                  finish=(lambda ob, db, a_=(256, OB[:, mi * 4 + h, q0:q0 + 256], ("ob", mi * 4 + h), sink): attn_finish(ob, db, *a_)))
                elif which == "B":
                    for n in range(8):
                        q0 = n * 128
                        qs = Q[:, h, q0:q0 + 128]
                        kbs = [dict(parts=[(KT[:, kh, j * 128:(j + 1) * 128], qs)], v=V[:, j, kh * 128:(kh + 1) * 128],
                                    pr=(0, 128)) for j in range(4)]
                        for kbk in (n - 1, n, n + 1):
                            if kbk < 0 or kbk > 7:
                                continue
                            m = None if kbk == n else (MASKS[:, 0, :] if kbk == n - 1 else MASKS[:, 1, :])
                            kbs.append(dict(parts=[(KT[:, kh, 512 + kbk * 128:512 + (kbk + 1) * 128], qs)],
                                            v=V[:, 4 + kbk, kh * 128:(kh + 1) * 128], pr=(0, 128), mask=m, mreads=["masks"]))
                        attn_unit(sc, 128, kbs, rd,
                                  finish=(lambda ob, db, a_=(128, OB[:, mi * 4 + h, q0:q0 + 128], ("ob", mi * 4 + h), sink): attn_finish(ob, db, *a_)))
                else:
                    for r in range(16):
                        q0 = r * 64
                        qs = Q[:, h, q0:q0 + 64]
                        kbs = [dict(parts=[(KT[:, kh, j * 128:(j + 1) * 128], qs)], v=V[:, j, kh * 128:(kh + 1) * 128],
                                    pr=(0, 128)) for j in range(4)]
                        rs = min(max(r - 4, 0), 8)
                        for m_ in range(rs // 2, (rs + 7) // 2 + 1):
                            lo = 64 if 2 * m_ < rs else 0
                            hi = 64 if 2 * m_ + 1 > rs + 7 else 128
                            idx = 2 * m_ - r + 7 + 1
                            kbs.append(dict(parts=[(KT[:, kh, 512 + m_ * 128:512 + (m_ + 1) * 128], qs)],
                                            v=V[lo:hi, 4 + m_, kh * 128:(kh + 1) * 128], pr=(lo, hi),
                                            mask=ETAB[lo:hi, h, idx, :], mreads=["etab"]))
                        attn_unit(sc, 64, kbs, rd,
                                  finish=(lambda ob, db, a_=(64, OB[:, mi * 4 + h, q0:q0 + 64], ("ob", mi * 4 + h), sink): attn_finish(ob, db, *a_)))
            else:
                scd = 64 ** -0.5
                units = [(s * 256, 256, [2 * s, 2 * s + 1]) for s in range(4)] if ph == "P" else \
                        [(t * TT, TT, list(range(nkt))) for t in range(2)]
                for (q0, nq, kts) in units:
                    for a in range(2):
                        pa = slice(a * 64, (a + 1) * 64)
                        kbs = [dict(parts=[(KT[pa, kh, j * 128:(j + 1) * 128], Q[pa, h, q0:q0 + nq])],
                                    v=V[:, j, kh * 128:(kh + 1) * 128], pr=(0, 128)) for j in kts]
                        fin = None
                        if a == 1:
                            fin = (lambda ob, db, nq=nq, dst=OB[:, mi * 4 + h, q0:q0 + nq], tok=("ob", mi * 4 + h):
                                   diff_finish(l, [(4, 5), (6, 7)], nq, dst, tok))
                        attn_unit(scd, nq, kbs, rd, banks=((4, 5) if a == 0 else (6, 7)), finish=fin)

    def diff_lambda(l):
        lam = lambda k: T1[:, 64 * k:64 * (k + 1)]
        sp_load(T1[:, 0:256], small[l, :, SM_LAM:SM_LAM + 256], "t1")
        S.add(DVE, I("tensor_tensor", out=T0[:, 0:64], in0=lam(0), in1=lam(1), op=ALU.mult), reads=["t1"], writes=["t0"])
        S.add(DVE, I("tensor_reduce", out=MISC[:, 4:5], in_=T0[:, 0:64], axis=mybir.AxisListType.X, op=ALU.add),
              reads=["t0"], writes=["misc"])
        S.add(DVE, I("tensor_tensor", out=T0[:, 0:64], in0=lam(2), in1=lam(3), op=ALU.mult), reads=["t1", "misc"], writes=["t0"])
        S.add(DVE, I("tensor_reduce", out=MISC[:, 5:6], in_=T0[:, 0:64], axis=mybir.AxisListType.X, op=ALU.add),
              reads=["t0"], writes=["misc"])
        S.add(ACT, I("activation", out=MISC[:, 6:8], in_=MISC[:, 4:6], func=AF.Exp), reads=["misc"], writes=["misc"])
        S.add(DVE, I("scalar_tensor_tensor", out=MISC[:, 8:9], in0=MISC[:, 7:8], scalar=-lambda_init(l), in1=MISC[:, 6:7],
                                                    op0=ALU.add, op1=ALU.subtract), reads=["misc"], writes=["misc"])
        S.add(DVE, I("tensor_scalar", out=MISC[:, 9:10], in0=SMALL[:, SM_SUBG:SM_SUBG + 1], scalar1=1.0 - lambda_init(l),
                                             scalar2=1.0, op0=ALU.mult, op1=ALU.mult), reads=["small", "misc"], writes=["misc"])

    def diff_finish(l, res, nq, dst, dtok):
        (o1, d1), (o2, d2) = res
        S.add(ACT, I("activation", out=T0[:, 0:nq], in_=PS[d1][:, 0:nq], func=AF.Ln), reads=[("ps", d1)], writes=["t0"])
        S.add(ACT, I("activation", out=T0[:, 0:nq], in_=T0[:, 0:nq], func=AF.Exp, scale=-1.0), reads=["t0"], writes=["t0"])
        S.add(DVE, I("tensor_tensor", out=T0[:, 0:nq], in0=PS[o1][:, 0:nq], in1=T0[:, 0:nq], op=ALU.mult),
              reads=[("ps", o1), "t0"], writes=["t0"])
        S.add(ACT, I("activation", out=T1[:, 0:nq], in_=PS[d2][:, 0:nq], func=AF.Ln), reads=[("ps", d2)], writes=["t1"])
        S.add(ACT, I("activation", out=T1[:, 0:nq], in_=T1[:, 0:nq], func=AF.Exp, scale=-1.0), reads=["t1"], writes=["t1"])
        S.add(DVE, I("tensor_tensor", out=T1[:, 0:nq], in0=PS[o2][:, 0:nq], in1=T1[:, 0:nq], op=ALU.mult),
              reads=[("ps", o2), "t1"], writes=["t1"])
        S.add(DVE, I("scalar_tensor_tensor", out=T0[:, 0:nq], in0=T1[:, 0:nq], scalar=MISC[:, 8:9], in1=T0[:, 0:nq],
                     op0=ALU.mult, op1=ALU.add), reads=["t0", "t1", "misc"], writes=["t0"])
        k = nsq()
        S.add(ACT, I("activation", out=SQ[k][:, 0:nq], in_=T0[:, 0:nq], func=AF.Square), reads=["t0"], writes=[("sq", k)])
        sbk = 3
        mm_group(sbk, (0, 128), (0, nq), [(ONES, SQ[k][:, 0:nq])], [("sq", k), "ones"])
        rstd_from_bank(sbk, (0, 128), nq, 1.0 / 128, RSTD, "rstd")
        S.add(DVE, I("scalar_tensor_tensor", out=dst, in0=T0[:, 0:nq], scalar=MISC[:, 9:10], in1=RSTD[:, 0:nq],
                                                    op0=ALU.mult, op1=ALU.mult), reads=["t0", "rstd", "misc"], writes=[dtok])

    def merge(l):
        ACC = [RSTD, view(OFF_SCR + 6144, 2048, F32)]
        acct = ["rstd", "acc1"]
        for n in range(16):
            bt = None
            for i in range(4):
                gt = [wload(w_gate[l, (n * 4 + i) * 2 + g]) for g in range(2)]
                if i % 2 == 0:
                    bt = wload(w_br[l, n * 2 + i // 2])
                for t in range(2):
                    ts = slice(t * TT, (t + 1) * TT)
                    ga = nbank()
                    pairs = [(gt[kc // 8][0][:, (kc % 8) * 128:(kc % 8 + 1) * 128], H[:, kc, ts]) for kc in range(NCH)]
                    mm_group(ga, (0, 128), (0, TT), pairs, hreads(t) + [x[1] for x in gt])
                    gb = nbank((4, 5, 6, 7))
                    pairs = [(bt[0][:, (i % 2) * 512 + kc * 128:(i % 2) * 512 + (kc + 1) * 128], OB[:, i * 4 + kc, ts]) for kc in range(4)]
                    mm_group(gb, (0, 128), (0, TT), pairs, [("ob", i * 4 + kc) for kc in range(4)] + [bt[1]])
                    tb, tk = (T0, "t0") if t == 0 else (T1, "t1")
                    S.add(ACT, I("activation", out=tb, in_=PS[ga][:, :], func=AF.Sigmoid), reads=[("ps", ga)], writes=[tk])
                    if i == 0:
                        S.add(DVE, I("tensor_tensor", out=ACC[t], in0=PS[gb][:, :], in1=tb, op=ALU.mult),
                              reads=[("ps", gb), tk], writes=[acct[t], ("sq", 0), ("sq", 1)] if t == 1 else [acct[t]])
                    else:
                        S.add(DVE, I("tensor_tensor", out=tb, in0=PS[gb][:, :], in1=tb, op=ALU.mult),
                              reads=[("ps", gb), tk], writes=[tk])
                        if i < 3:
                            S.add(DVE, I("tensor_tensor", out=ACC[t], in0=ACC[t], in1=tb, op=ALU.add),
                                  reads=[acct[t], tk], writes=[acct[t]])
                        else:
                            S.add(DVE, I("tensor_tensor", out=MIXB[:, n, ts], in0=ACC[t], in1=tb, op=ALU.add),
                                  reads=[acct[t], tk], writes=[("mix", n, t)] + ([("sq", 0), ("sq", 1)] if t == 1 else []))

    def out_norm_res(l, wsrc, ntile, kcs, src_fn, src_reads_fn, gvec):
        for t in range(2):
            ts = slice(t * TT, (t + 1) * TT)
            sbk = 7
            for n in range(16):
                bank = nbank((0, 1, 2, 3, 4, 5))
                kc = 0
                for g in range(ntile):
                    tl = wload(wsrc[l, n * ntile + g], kcs[g] * 128)
                    pairs = []
                    for kk in range(kcs[g]):
                        pairs.append((tl[0][:, kk * 128:(kk + 1) * 128], src_fn(kc, ts)))
                        kc += 1
                    mm_group(bank, (0, 128), (0, TT), pairs, src_reads_fn(t) + [tl[1]], first=(g == 0), last=(g == ntile - 1))
                S.add(ACT, I("activation", out=MO[:, n, :], in_=PS[bank][:, :], func=AF.Identity),
                      reads=[("ps", bank)], writes=[("h", n, 0), ("h", n, 1)])
                k = nsq()
                S.add(ACT, I("activation", out=SQ[k], in_=PS[bank][:, :], func=AF.Square),
                      reads=[("ps", bank)], writes=[("sq", k)])
                S.add(PE, I("matmul", PS[sbk][:, :], ONES, SQ[k], start=(n == 0), stop=(n == 15)),
                      reads=[("sq", k), "ones"], writes=[("ps", sbk)])
                if n == 0:
                    stage(f"o{t}_n0")
            stage(f"o{t}_mm")
            rstd_from_bank(sbk, (0, 128), TT, 1.0 / D_MODEL, RSTD, "rstd")
            stage(f"o{t}_rs")
            for n in range(16):
                tb, tk = (T0, "t0") if n % 2 == 0 else (T1, "t1")
                S.add(DVE, I("scalar_tensor_tensor",
                    out=tb, in0=MO[:, n, :], scalar=VEC[:, gvec, n:n + 1], in1=RSTD, op0=ALU.mult, op1=ALU.mult),
                    reads=[("h", n, 0), ("h", n, 1), "vec", "rstd"], writes=[tk])
                S.add(DVE, I("tensor_tensor", out=X[:, n, ts], in0=X[:, n, ts], in1=tb, op=ALU.add),
                      reads=[tk, ("x", n, t)], writes=[("x", n, t)])
                if n == 0:
                    stage(f"o{t}_x0")
            stage(f"o{t}_x")

    def ffn_gate_up(l, mod_next=None):
        if mod_next is not None:
            sp_load(RSTD[:, 0:96], small[mod_next, :, SM_BADA:SM_BADA + 96], "rstd")
        for jc in range(NFC):
            if mod_next is not None and jc < 24:
                compute_mod(mod_next, jbs=[jc], small_tile=RSTD[:, 0:96], small_tok="rstd",
                            rowtmp=view(OFF_SCR + 6144, 2048, F32), rowtoks=(("sq", 0), ("sq", 1)))
            gtl = [wload(w_gu[l, (jc * 2 + 0) * 2 + g]) for g in range(2)]
            utl = [wload(w_gu[l, (jc * 2 + 1) * 2 + g]) for g in range(2)]
            for t in range(2):
                ts = slice(t * TT, (t + 1) * TT)
                ga = nbank()
                pairs = [(gtl[kc // 8][0][:, (kc % 8) * 128:(kc % 8 + 1) * 128], H[:, kc, ts]) for kc in range(NCH)]
                mm_group(ga, (0, 128), (0, TT), pairs, hreads(t) + [x[1] for x in gtl])
                ub = nbank((4, 5, 6, 7))
                pairs = [(utl[kc // 8][0][:, (kc % 8) * 128:(kc % 8 + 1) * 128], H[:, kc, ts]) for kc in range(NCH)]
                mm_group(ub, (0, 128), (0, TT), pairs, hreads(t) + [x[1] for x in utl])
                tb, tk = (T0, "t0") if t == 0 else (T1, "t1")
                S.add(ACT, I("activation", out=tb, in_=PS[ga][:, :], func=AF.Silu),
                      reads=[("ps", ga)], writes=[tk])
                S.add(DVE, I("tensor_tensor", out=ACTB[:, jc, ts], in0=PS[ub][:, :], in1=tb, op=ALU.mult),
                      reads=[("ps", ub), tk], writes=[("actb", jc, t)])

    class _Stop(Exception):
        pass

    def stage(name):
        if stop == name:
            raise _Stop()

    def run_all():
        for ph in phases:
            j = 0 if ph == "P" else 1
            for c in range(NCH):
                for t in range(2):
                    sp_load(X[:, c, t * TT:(t + 1) * TT], xin[ph][c * 128:(c + 1) * 128, t * TT:(t + 1) * TT], ("x", c, t))
            stage("load")
            for l in layers:
                load_small(l)
                if ph == phases[0] and l == layers[0]:
                    compute_mod(l)
                stage("mod")
                layer_vectors(l, j)
                norm_mod(l, j, 0, 0)
                stage("norm")
                attn_flush()
                S.barrier()
                mixer_A(l, ph)
                stage("A")
                attn_flush()
                S.barrier()
                mixer_BCD(l, ph, "B")
                stage("B")
                attn_flush()
                S.barrier()
                mixer_BCD(l, ph, "C")
                stage("C")
                attn_flush()
                S.barrier()
                mixer_BCD(l, ph, "D")
                stage("D")
                attn_flush()
                S.barrier()
                merge(l)
                stage("merge")
                out_norm_res(l, w_out, 2, (8, 8), lambda kc, ts: MIXB[:, kc, ts],
                             lambda t: [("mix", n, t) for n in range(16)], 1)
                stage("out")
                norm_mod(l, j, 2, 48)
                attn_flush()
                S.barrier()
                nxt = layers[layers.index(l) + 1] if (ph == phases[0] and layers.index(l) + 1 < len(layers)) else None
                ffn_gate_up(l, mod_next=nxt)
                stage("gu")
                out_norm_res(l, w_dn, 6, (8, 8, 8, 8, 8, 4), lambda kc, ts: ACTB[:, kc, ts],
                             lambda t: [("actb", jc, t) for jc in range(NFC)], 3)
            for c in range(NCH):
                for t in range(2):
                    store(yout[ph][c * 128:(c + 1) * 128, t * TT:(t + 1) * TT], X[:, c, t * TT:(t + 1) * TT], ("x", c, t))

    try:
        run_all()
    except _Stop:
        for c in range(NCH):
            for t in range(1 if os.environ.get('DBG_HALFSTORE') else 2):
                store(yout[phases[0]][c * 128:(c + 1) * 128, t * TT:(t + 1) * TT], X[:, c, t * TT:(t + 1) * TT], ("x", c, t))
    S.emit_all(nc, final_wait_ops=out_ops)
    return nc


def _tile_w(w, col_blocks, kcs):
    K = w.shape[0]
    out = []
    for (c0, ncols) in col_blocks:
        k0 = 0
        for kc in kcs:
            tl = np.zeros((128, 8, 128), np.float32)
            blk = w[k0 * 128:(k0 + kc) * 128, c0:c0 + ncols].reshape(kc, 128, ncols).transpose(1, 0, 2)
            tl[:, :kc, :ncols] = blk
            out.append(tl.reshape(128, 1024))
            k0 += kc
        assert k0 * 128 == K
    return np.stack(out)


def _prep_shared(inp):
    L = DEPTH
    sh = {}
    f = lambda a: np.ascontiguousarray(np.asarray(a, dtype=np.float32))
    w_ada = f(inp["w_ada"])
    t_ada = np.zeros((L, 24 * 8, 128, 1024), np.float32)
    for l in range(L):
        v = w_ada[l].reshape(8, 2, 128, 24, 512)
        t_ada[l] = v.transpose(3, 0, 2, 1, 4).reshape(24 * 8, 128, 1024)
    sh["w_ada_t"] = t_ada
    w_in = f(inp["w_in"])
    sh["w_in_t"] = np.stack([_tile_w(w_in[l], [(c0, nc_) for (_, _, c0, nc_) in WIN_BLOCKS], (8, 8)) for l in range(L)])
    w_uq = f(inp["w_mla_uq"])
    uq = np.zeros((L, 128, 8, 3, 128), np.float32)
    for l in range(L):
        for h in range(4):
            uq[l, :, h, :, :] = w_uq[l][:, h * 192:h * 192 + 128].reshape(3, 128, 128).transpose(1, 0, 2)
            uq[l, :, 4 + h, :, :64] = w_uq[l][:, h * 192 + 128:h * 192 + 192].reshape(3, 128, 64).transpose(1, 0, 2)
    sh["w_uq_t"] = uq.reshape(L, 128, 8 * 384)
    w_ukv = f(inp["w_mla_ukv"]).reshape(L, 128, 4, 2, 128)
    sh["w_ukv_t"] = np.ascontiguousarray(w_ukv.transpose(0, 1, 3, 2, 4)).reshape(L, 128, 1024)
    w_gate = f(inp["w_mix_gate"])
    sh["w_gate_t"] = np.stack([
        np.stack([_tile_w(w_gate[l, i], [(n * 128, 128) for n in range(16)], (8, 8)).reshape(16, 2, 128, 1024) for i in range(4)],
                 axis=1).reshape(16 * 4 * 2, 128, 1024) for l in range(L)])
    w_br = f(inp["w_branch"])
    br = np.zeros((L, 16, 2, 128, 2, 4, 128), np.float32)
    for l in range(L):
        for i in range(4):
            v = w_br[l, i].reshape(4, 128, 16, 128)
            br[l, :, i // 2, :, i % 2, :, :] = v.transpose(2, 1, 0, 3)
    sh["w_br_t"] = br.reshape(L, 32, 128, 1024)
    w_out = f(inp["w_out"])
    sh["w_out_t"] = np.stack([_tile_w(w_out[l], [(n * 128, 128) for n in range(16)], (8, 8)) for l in range(L)])
    wg, wu = f(inp["w_ffn_gate"]), f(inp["w_ffn_up"])
    gu = []
    for l in range(L):
        tg = _tile_w(wg[l], [(n * 128, 128) for n in range(NFC)], (8, 8)).reshape(NFC, 1, 2, 128, 1024)
        tu = _tile_w(wu[l], [(n * 128, 128) for n in range(NFC)], (8, 8)).reshape(NFC, 1, 2, 128, 1024)
        gu.append(np.concatenate([tg, tu], axis=1).reshape(NFC * 4, 128, 1024))
    sh["w_gu_t"] = np.stack(gu)
    wd = f(inp["w_ffn_down"])
    sh["w_dn_t"] = np.stack([_tile_w(wd[l], [(n * 128, 128) for n in range(16)], (8, 8, 8, 8, 8, 4)) for l in range(L)])
    sm = np.zeros((L, 128, NSMALL), np.float32)
    fm = lambda v, n: v.reshape(n, 128).T
    for l in range(L):
        for k, nm in enumerate(("mix_pre_g", "mix_post_g", "ffn_pre_g", "ffn_post_g")):
            sm[l, :, SM_GAIN + 16 * k:SM_GAIN + 16 * (k + 1)] = fm(f(inp[nm])[l], 16)
        sm[l, :, SM_BADA:SM_BADA + 96] = fm(f(inp["b_ada"])[l], 96)
        sm[l, :, SM_QG:SM_QG + 3] = fm(f(inp["mla_q_norm_g"])[l], 3)
        sm[l, :, SM_KVG] = f(inp["mla_kv_norm_g"])[l]
        sm[l, :, SM_SUBG] = f(inp["diff_subln_g"])[l]
        sm[l, :, SM_SINK:SM_SINK + 4] = f(inp["swa_sink"])[l][None, :]
        for k, nm in enumerate(("diff_lq1", "diff_lk1", "diff_lq2", "diff_lk2")):
            sm[l, :, SM_LAM + 64 * k:SM_LAM + 64 * (k + 1)] = f(inp[nm])[l][None, :]
    sh["small"] = sm
    p128 = np.zeros((128, 128), np.float32)
    p64 = np.zeros((128, 128), np.float32)
    for p in range(128):
        p128[p, (p + 64) % 128] = 1.0
        p64[p, (p // 64) * 64 + (p % 64 + 32) % 64] = 1.0
    sh["cperm"] = np.stack([p128, p64])
    sh["ceye"] = np.eye(2, dtype=np.float32)
    tpos = np.arange(T)
    row = (tpos // 64).astype(np.float32)
    col = (tpos % 64).astype(np.float32)
    tabs = []
    for r in (128, 64):
        quarter = r // 4
        inv = (1.0 / (10000.0 ** (np.arange(quarter, dtype=np.float32) / quarter))).astype(np.float32)
        ang = np.concatenate([row[:, None] * inv, col[:, None] * inv], axis=-1)
        cs, sn = np.cos(ang).T.astype(np.float32), np.sin(ang).T.astype(np.float32)
        C = np.concatenate([cs, cs], axis=0)
        Sg = np.concatenate([-sn, sn], axis=0)
        reps = 128 // r
        tabs += [np.tile(C, (reps, 1)), np.tile(Sg, (reps, 1))]
    sh["ropetab"] = np.stack(tabs).astype(np.float32)
    jj = np.arange(128)[:, None]
    ii = np.arange(128)[None, :]
    maskL = (jj >= ii).astype(np.float32)
    maskU = (jj <= ii).astype(np.float32)
    ck = np.arange(64)[:, None]
    cq = np.arange(64)[None, :]
    cs_ = np.clip(cq - 8, 0, 48)
    colv = ((ck >= cs_) & (ck < cs_ + 16)).astype(np.float32)
    m3 = np.zeros((128, 128), np.float32)
    m3[:64, :64] = colv
    m3[64:, :64] = colv
    sh["cmask"] = np.stack([maskL, maskU, m3])
    rpb = f(inp["na_rpb"])
    dc = np.clip(ck - cq + 15, 0, 30)
    nab = np.zeros((L, 128, 4, 16, 64), np.float32)
    for i in range(16):
        if 0 <= i - 1 <= 14:
            nab[:, :64, :, i, :] = rpb[:, :, i - 1, :][:, :, dc].transpose(0, 2, 1, 3)
        if i <= 14:
            nab[:, 64:, :, i, :] = rpb[:, :, i, :][:, :, dc].transpose(0, 2, 1, 3)
    sh["nabias"] = nab.reshape(L, 128, 4 * 16 * 64)
    return sh


_PROG = {}


def prep_in_maps(inp, cores=range(8)):
    f = lambda a: np.ascontiguousarray(np.asarray(a, dtype=np.float32))
    sh = _prep_shared(inp)
    xp, xs = f(inp["x_prompt"]), f(inp["x_sample"])
    c, c_ctx = f(inp["c"]), f(inp["c_ctx"])
    caches = {k: f(inp[k]) for k in ("cache_mla_ckv", "cache_mla_krope", "cache_swa_k", "cache_swa_v", "cache_na_k",
                                     "cache_na_v", "cache_diff_k", "cache_diff_v")}
    in_maps = []
    for b in cores:
        m = dict(sh)
        m["xP"] = np.ascontiguousarray(xp[4 * b:4 * b + 4].reshape(T, D_MODEL).T)
        m["xS"] = np.ascontiguousarray(xs[b].T)
        cv = np.stack([c_ctx, c[b]], axis=-1).reshape(16, 128, 2).transpose(1, 0, 2).reshape(128, 32)
        m["cvec"] = np.ascontiguousarray(cv)
        m["kc_mla"] = np.ascontiguousarray(caches["cache_mla_ckv"][b].transpose(0, 2, 1))
        m["kc_kro"] = np.ascontiguousarray(caches["cache_mla_krope"][b].transpose(0, 2, 1))
        m["kc_b"] = np.ascontiguousarray(caches["cache_swa_k"][b].reshape(DEPTH, 512, 256).transpose(0, 2, 1))
        m["kc_c"] = np.ascontiguousarray(caches["cache_na_k"][b].reshape(DEPTH, 512, 512).transpose(0, 2, 1))
        m["kc_d"] = np.ascontiguousarray(caches["cache_diff_k"][b].reshape(DEPTH, 512, 512).transpose(0, 2, 1))
        m["vc_b"] = np.ascontiguousarray(caches["cache_swa_v"][b].reshape(DEPTH, 512, 256))
        m["vc_c"] = np.ascontiguousarray(caches["cache_na_v"][b].reshape(DEPTH, 512, 512))
        m["vc_d"] = np.ascontiguousarray(caches["cache_diff_v"][b].reshape(DEPTH, 512, 512))
        in_maps.append(m)
    return in_maps


def kernel(**inp):
    in_maps = prep_in_maps(inp)
    if "nc" not in _PROG:
        _PROG["nc"] = build_program()
    res = run_bass_kernel_spmd(_PROG["nc"], in_maps, core_ids=list(range(8)))
    R = res.results
    cat = lambda k: [np.asarray(R[b][k], dtype=np.float32) for b in range(8)]
    y = np.concatenate([r.T.reshape(4, 256, D_MODEL) for r in cat("yP")], axis=0)
    z = np.stack([r.T for r in cat("yS")], axis=0)

    def kout(key, heads, d):
        outs = []
        for r in cat(key):
            v = r.reshape(DEPTH, r.shape[1], 4, 256).transpose(2, 0, 3, 1)
            outs.append(v.reshape(4, DEPTH, 256, heads, d) if heads else v)
        return np.ascontiguousarray(np.concatenate(outs, axis=0))

    def vout(key, heads, d):
        outs = []
        for r in cat(key):
            v = r.reshape(DEPTH, 4, 256, heads, d).transpose(1, 0, 2, 3, 4)
            outs.append(v)
        return np.ascontiguousarray(np.concatenate(outs, axis=0))
    new_ckv = kout("o_ckv", 0, 128)
    new_kro = kout("o_kro", 0, 64)
    return (np.ascontiguousarray(y), np.ascontiguousarray(z), new_ckv, new_kro,
            kout("o_kb", 2, 128), vout("o_vb", 2, 128), kout("o_kc", 4, 128), vout("o_vc", 4, 128),
            kout("o_kd", 4, 128), vout("o_vd", 4, 128))
```

```python
import contextlib
import math
import numpy as np
import concourse.bass as bass
import concourse.mybir as mybir
from concourse.bass_utils import run_bass_kernel_spmd

F32 = mybir.dt.float32
BF16 = mybir.dt.bfloat16
U8 = mybir.dt.uint8
AF = mybir.ActivationFunctionType
ALU = mybir.AluOpType

PE, ACT, DVE, POOL, SP = "pe", "act", "dve", "pool", "sp"
ENGS = (PE, ACT, DVE, POOL, SP)
SEM_CAP = 16000
N_DMA_SEMS = 24

D_MODEL = 2048
NCH = 16
T = 1024
TT = 512
DEPTH = 2
FFN_H = 5632
NFC = 44
EPS = 1e-6
IN_WIDTH = 4672
SLOT_ELEMS = 1024
import os
DBG_NOKOUT = bool(os.environ.get('DBG_NOKOUT'))
NSLOT = 6


def lambda_init(l):
    return 0.8 - 0.6 * math.exp(-0.3 * l)


class Op:
    __slots__ = ("eng", "emit", "deps", "signal", "ev", "is_dma", "dsem", "dval")

    def __init__(self, eng, emit, is_dma=False):
        self.eng = eng
        self.emit = emit
        self.deps = []
        self.signal = False
        self.ev = None
        self.is_dma = is_dma
        self.dsem = None
        self.dval = None


class Sched:
    def __init__(self):
        self.ops = {e: [] for e in ENGS}
        self.lastw = {}
        self.readers = {}
        self.dma_rr = {POOL: 0, SP: 0}
        self.dma_last = [None] * N_DMA_SEMS
        self.dma_cnt = [0] * N_DMA_SEMS
        self.fence = []

    def barrier(self):
        self.fence = [self.ops[e][-1] for e in (PE, ACT, DVE) if self.ops[e]]
        self.fence += [op for op in self.dma_last if op is not None and op.eng == SP]

    ALIAS = {"t0": [("t0g", g) for g in range(8)], "t1": [("t1g", g) for g in range(8)],
             "rstd": [("rsg", g) for g in range(8)]}

    def _expand(self, toks):
        out = []
        for t in toks:
            if type(t) is str and t in self.ALIAS:
                out.extend(self.ALIAS[t])
            else:
                out.append(t)
        return out

    def add(self, eng, emit, reads=(), writes=(), dma=False, nofence=False):
        reads = self._expand(reads)
        writes = self._expand(writes)
        op = Op(eng, emit, dma)
        deps = [] if nofence else list(self.fence)
        for t in reads:
            w = self.lastw.get(t)
            if w is not None:
                deps.append(w)
            if type(t) is tuple and t[0] == "ps":
                deps.extend(r for r in self.readers.get(t, ()) if r.eng != eng)
        for t in writes:
            w = self.lastw.get(t)
            if w is not None:
                deps.append(w)
            deps.extend(self.readers.get(t, ()))
        if dma:
            half = N_DMA_SEMS // 2
            r = self.dma_rr[eng]
            self.dma_rr[eng] = (r + 1) % half
            s = r + (half if eng == SP else 0)
            prev = self.dma_last[s]
            if prev is not None:
                deps.append(prev)
            self.dma_last[s] = op
            self.dma_cnt[s] += 1
            op.dsem = s
            op.dval = 16 * self.dma_cnt[s]
        seen = set()
        for d in deps:
            if d is op or id(d) in seen:
                continue
            seen.add(id(d))
            if (not d.is_dma) and d.eng == eng and eng in (PE, SP):
                continue
            op.deps.append(d)
            d.signal = True
        for t in reads:
            self.readers.setdefault(t, []).append(op)
        for t in writes:
            self.lastw[t] = op
            self.readers[t] = []
        self.ops[eng].append(op)
        return op

    def emit_all(self, nc, final_wait_ops=()):
        for d in final_wait_ops:
            d.signal = True
        eng_sems = {}
        for e in ENGS:
            cnt = 0
            k = 0
            for op in self.ops[e]:
                if op.is_dma:
                    op.ev = ("dma", op.dsem, op.dval)
                elif op.signal:
                    if cnt >= SEM_CAP:
                        k += 1
                        cnt = 0
                    cnt += 1
                    op.ev = (e, k, cnt)
            eng_sems[e] = k + 1
        with contextlib.ExitStack() as st:
            sems = {}
            for e in ENGS:
                for k in range(eng_sems[e]):
                    sems[(e, k)] = st.enter_context(nc.semaphore(f"s_{e}_{k}"))
            for s in range(N_DMA_SEMS):
                sems[("dma", s)] = st.enter_context(nc.semaphore(f"s_dma_{s}"))
            block = st.enter_context(nc.Block())
            tail = list(final_wait_ops)

            def run(e, eng):
                waited = {}
                for op in self.ops[e]:
                    for d in op.deps:
                        key = (d.ev[0], d.ev[1])
                        val = d.ev[2]
                        if waited.get(key, 0) < val:
                            eng.wait_ge(sems[key], val)
                            waited[key] = val
                    ins = op.emit(eng)
                    if op.is_dma:
                        ins.then_inc(sems[("dma", op.dsem)], 16)
                    elif op.signal:
                        ins.then_inc(sems[(op.ev[0], op.ev[1])], 1)
                if e == SP:
                    for d in tail:
                        key = (d.ev[0], d.ev[1])
                        if waited.get(key, 0) < d.ev[2]:
                            eng.wait_ge(sems[key], d.ev[2])
                            waited[key] = d.ev[2]

            @block.tensor
            def _(eng):
                run(PE, eng)

            @block.scalar
            def _(eng):
                run(ACT, eng)

            @block.vector
            def _(eng):
                run(DVE, eng)

            @block.gpsimd
            def _(eng):
                run(POOL, eng)

            @block.sync
            def _(eng):
                run(SP, eng)


def I(method, *args, **kwargs):
    return lambda e: getattr(e, method)(*args, **kwargs)


SM_GAIN = 0
SM_BADA = 64
SM_QG = 160
SM_KVG = 163
SM_SUBG = 164
SM_SINK = 165
SM_LAM = 169
NSMALL = 169 + 256

_o = np.cumsum([0, 384, 128, 64, 512, 256, 256, 512, 512, 512, 512, 512, 512])
COL = dict(qdn=_o[0], ckv=_o[1], kro=_o[2], qb=_o[3], kb=_o[4], vb=_o[5], qc=_o[6], kc=_o[7],
           vc=_o[8], qd=_o[9], kd=_o[10], vd=_o[11])


def win_blocks():
    bl = []
    for i in range(3):
        bl.append(("qdn", i, COL["qdn"] + 128 * i, 128))
    bl.append(("ckv", 0, COL["ckv"], 128))
    bl.append(("kro", 0, COL["kro"], 64))
    for nm, n in (("qb", 4), ("kb", 2), ("vb", 2), ("qc", 4), ("kc", 4), ("vc", 4),
                  ("qd", 4), ("kd", 4), ("vd", 4)):
        for i in range(n):
            bl.append((nm, i, COL[nm] + 128 * i, 128))
    return bl


WIN_BLOCKS = win_blocks()
WIN_IDX = {(nm, i): k for k, (nm, i, _, _) in enumerate(WIN_BLOCKS)}


def build_program(layers=(0, 1), phases=("P", "S"), stop=None):
    nc = bass.Bass("TRN2", target_bir_lowering=False)
    S = Sched()
    L = DEPTH

    def din(name, shape):
        return nc.dram_tensor(name, list(shape), F32, kind="ExternalInput").ap()

    def dout(name, shape):
        return nc.dram_tensor(name, list(shape), F32, kind="ExternalOutput").ap()

    xin = {"P": din("xP", [D_MODEL, T]), "S": din("xS", [D_MODEL, T])}
    cvec = din("cvec", [128, 32])
    small = din("small", [L, 128, NSMALL])
    cperm = din("cperm", [2, 128, 128])
    ceye = din("ceye", [2, 2])
    ropetab = din("ropetab", [4, 128, T])
    cmask = din("cmask", [3, 128, 128])
    nabias = din("nabias", [L, 128, 4 * 16 * 64])
    w_ada = din("w_ada_t", [L, 24 * 8, 128, 1024])
    w_in = din("w_in_t", [L, len(WIN_BLOCKS) * 2, 128, 1024])
    w_uq = din("w_uq_t", [L, 128, 8 * 384])
    w_ukv = din("w_ukv_t", [L, 128, 1024])
    w_gate = din("w_gate_t", [L, 16 * 4 * 2, 128, 1024])
    w_br = din("w_br_t", [L, 16 * 2, 128, 1024])
    w_out = din("w_out_t", [L, 16 * 2, 128, 1024])
    w_gu = din("w_gu_t", [L, NFC * 2 * 2, 128, 1024])
    w_dn = din("w_dn_t", [L, 16 * 6, 128, 1024])
    kc_mla = din("kc_mla", [L, 128, 512])
    kc_kro = din("kc_kro", [L, 64, 512])
    kc_b = din("kc_b", [L, 256, 512])
    kc_c = din("kc_c", [L, 512, 512])
    kc_d = din("kc_d", [L, 512, 512])
    vc_b = din("vc_b", [L, 512, 256])
    vc_c = din("vc_c", [L, 512, 512])
    vc_d = din("vc_d", [L, 512, 512])
    yout = {"P": dout("yP", [D_MODEL, T]), "S": dout("yS", [D_MODEL, T])}
    o_ckv = dout("o_ckv", [L, 128, T])
    o_kro = dout("o_kro", [L, 64, T])
    o_kb = dout("o_kb", [L, 256, T])
    o_vb = dout("o_vb", [L, T, 256])
    o_kc = dout("o_kc", [L, 512, T])
    o_vc = dout("o_vc", [L, T, 512])
    o_kd = dout("o_kd", [L, 512, T])
    o_vd = dout("o_vd", [L, T, 512])
    out_ops = []

    OFF_X = 0
    OFF_H = 65536
    OFF_BIG = 98304
    BIG_SZ = 90112
    OFF_W = OFF_BIG + BIG_SZ
    OFF_SCR = OFF_W + NSLOT * 2048
    SCR_SZ = 8192
    OFF_CST = OFF_SCR + SCR_SZ
    CST_SZ = 3936
    TOTAL = OFF_CST + CST_SZ
    arena = nc.alloc_sbuf_tensor("arena", [128, TOTAL], U8)

    def view(off, nbytes, dt):
        return arena[:, off:off + nbytes].bitcast(dt)

    X = view(OFF_X, 65536, F32).rearrange("p (c t) -> p c t", c=NCH)
    H = view(OFF_H, 32768, BF16).rearrange("p (c t) -> p c t", c=NCH)
    MO = view(OFF_H, 32768, F32).rearrange("p (c t) -> p c t", c=NCH)
    ACTB = view(OFF_BIG, NFC * 2048, BF16).rearrange("p (c t) -> p c t", c=NFC)
    OB = view(OFF_BIG, 32768, BF16).rearrange("p (c t) -> p c t", c=16)
    MIXB = view(OFF_BIG + 32768, 32768, BF16).rearrange("p (c t) -> p c t", c=16)
    OFF_ROPE = OFF_BIG + 32768
    ROPE_C = view(OFF_ROPE, 4096, F32)
    ROPE_S = view(OFF_ROPE + 4096, 4096, F32)
    ETAB = view(OFF_ROPE, 8192, BF16).rearrange("p (h i c) -> p h i c", h=4, i=16)
    OFF_QKV = OFF_BIG + 40960
    QKV_SZ = 45056
    OFF_PST = OFF_QKV + QKV_SZ
    PST = [view(OFF_PST + 1024 * i, 1024, BF16) for i in range(4)]
    OSTG = [view(OFF_QKV + 36864 + 2048 * i, 2048, F32) for i in range(4)]
    WS = [view(OFF_W + 2048 * i, 2048, BF16) for i in range(NSLOT)]
    T0 = view(OFF_SCR, 2048, F32)
    T1 = view(OFF_SCR + 2048, 2048, F32)
    RSTD = view(OFF_SCR + 4096, 2048, F32)
    SQ = [view(OFF_SCR + 6144 + 1024 * i, 1024, BF16) for i in range(2)]
    co = [OFF_CST]

    def calloc(nbytes, dt):
        v = view(co[0], nbytes, dt)
        co[0] += nbytes
        assert co[0] <= TOTAL
        return v

    ONES = calloc(256, BF16)
    PERM = [calloc(256, BF16), calloc(256, BF16)]
    EYE2 = calloc(8, F32)
    CEPS = calloc(8, F32)
    SC = calloc(64, BF16).rearrange("p (c j) -> p c j", j=2)
    MODT = [calloc(768, F32).rearrange("p (c j) -> p c j", j=2) for _ in range(L)]
    SMALL = calloc(SM_LAM * 4, F32)
    VECF = calloc(4 * 64, F32)
    VEC = VECF.rearrange("p (v c) -> p v c", v=4)
    CV = VECF[:, 0:32]
    MASKS = calloc(512, BF16).rearrange("p (m c) -> p m c", m=2)
    MISC = calloc(64, F32)
    PS = [nc.alloc_psum_tensor(f"ps{i}", [128, 512], F32) for i in range(8)]

    st = {"slot": 0, "bank": 0, "pst": 0, "tq": 0, "sq": 0, "sring": 0, "ostg": 0}

    def nostg():
        k = st["ostg"] % 4
        st["ostg"] += 1
        return OSTG[k], ("ostg", k)

    def wload(src_ap, nelem=SLOT_ELEMS):
        k = st["slot"]
        st["slot"] = (k + 1) % NSLOT
        dst = WS[k][:, 0:nelem]
        src = src_ap if nelem == SLOT_ELEMS else src_ap[:, 0:nelem]
        S.add(POOL, I("dma_start", out=dst, in_=src), writes=[("w", k)], dma=True, nofence=True)
        return WS[k], ("w", k)

    def nbank(ring=(0, 1, 2, 3)):
        b = ring[st["bank"] % len(ring)]
        st["bank"] += 1
        return b

    def npst():
        k = st["pst"] % 4
        st["pst"] += 1
        return k

    def nsq():
        k = st["sq"] % 2
        st["sq"] += 1
        return k

    def mm_group(bank, pr, cols, pairs, reads, extra_writes=(), first=True, last=True):
        n = len(pairs)

        def emit(e):
            ins = None
            for i, (lt, rh) in enumerate(pairs):
                ins = e.matmul(PS[bank][pr[0]:pr[1], cols[0]:cols[1]], lt, rh,
                               start=(first and i == 0), stop=(last and i == n - 1))
            return ins
        return S.add(PE, emit, reads=list(reads), writes=[("ps", bank)] + list(extra_writes))

    def sp_load(dst, src, tok):
        return S.add(SP, I("dma_start", out=dst, in_=src), writes=[tok], dma=True)

    def pool_load(dst, src, tok):
        return S.add(POOL, I("dma_start", out=dst, in_=src), writes=[tok], dma=True)

    def store(dst, src, tok):
        op = S.add(SP, I("dma_start", out=dst, in_=src), reads=[tok], dma=True)
        out_ops.append(op)
        return op

    S.add(DVE, I("memset", ONES, 1.0), writes=["ones"])
    S.add(DVE, I("memset", CEPS, EPS), writes=["ceps"])
    pool_load(PERM[0], cperm[0], "perm")
    pool_load(PERM[1], cperm[1], "perm")
    sp_load(EYE2[0:2, :], ceye, "eye2")
    sp_load(CV, cvec, "vec")
    S.add(ACT, I("activation", out=SC.rearrange("p c j -> p (c j)"), in_=CV, func=AF.Silu),
          reads=["vec"], writes=["sc"])
    pool_load(MASKS[:, 0, :], cmask[0], "masks")
    pool_load(MASKS[:, 1, :], cmask[1], "masks")

    def load_small(l):
        sp_load(SMALL, small[l, :, 0:SM_LAM], "small")

    def compute_mod(l, jbs=range(24), small_tile=None, small_tok="small", ring=(0, 1, 2, 3), rowtmp=None, rowtoks=("t0",)):
        sm_ = SMALL[:, SM_BADA:SM_BADA + 96] if small_tile is None else small_tile
        rt_ = T0 if rowtmp is None else rowtmp
        for jb in jbs:
            bank = nbank(ring)
            for g in range(8):
                tl = wload(w_ada[l, jb * 8 + g])
                pairs = [(SC[:, 2 * g + kk, :], tl[0][:, kk * 512:(kk + 1) * 512]) for kk in range(2)]
                mm_group(bank, (0, 2), (0, 512), pairs, ["sc", tl[1]], first=(g == 0), last=(g == 7))
            S.add(ACT, I("activation", out=rt_[0:2, :], in_=PS[bank][0:2, :], func=AF.Identity),
                  reads=[("ps", bank)], writes=list(rowtoks))
            b2 = nbank(ring)

            def emit(e, b2=b2):
                ins = None
                for c in range(4):
                    ins = e.matmul(PS[b2][:, 2 * c:2 * c + 2], rt_[0:2, c * 128:(c + 1) * 128], EYE2[0:2, :],
                                   start=True, stop=True)
                return ins
            S.add(PE, emit, reads=list(rowtoks) + ["eye2"], writes=[("ps", b2)])
            S.add(DVE, I("tensor_tensor",
                out=MODT[l][:, jb * 4:(jb + 1) * 4, :],
                in0=PS[b2][:, 0:8].rearrange("p (c j) -> p c j", j=2),
                in1=sm_[:, jb * 4:(jb + 1) * 4].unsqueeze(2).broadcast_to([128, 4, 2]),
                op=ALU.add), reads=[("ps", b2), small_tok], writes=[("modt", l)])

    def layer_vectors(l, j):
        m = MODT[l]
        gsl = lambda k: SMALL[:, SM_GAIN + 16 * k:SM_GAIN + 16 * (k + 1)]
        S.add(DVE, I("scalar_tensor_tensor", out=VEC[:, 0, :], in0=m[:, 16:32, j], scalar=1.0, in1=gsl(0),
                                                    op0=ALU.add, op1=ALU.mult),
              reads=[("modt", l), "small"], writes=["vec"])
        S.add(DVE, I("tensor_tensor", out=VEC[:, 1, :], in0=m[:, 32:48, j], in1=gsl(1), op=ALU.mult),
              reads=[("modt", l), "small"], writes=["vec"])
        S.add(DVE, I("scalar_tensor_tensor", out=VEC[:, 2, :], in0=m[:, 64:80, j], scalar=1.0, in1=gsl(2),
                                                    op0=ALU.add, op1=ALU.mult),
              reads=[("modt", l), "small"], writes=["vec"])
        S.add(DVE, I("tensor_tensor", out=VEC[:, 3, :], in0=m[:, 80:96, j], in1=gsl(3), op=ALU.mult),
              reads=[("modt", l), "small"], writes=["vec"])

    def rstd_from_bank(bank, pr, n, inv_n, dst, dst_tok):
        p0, p1 = pr
        S.add(ACT, I("activation", out=T1[p0:p1, 0:n], in_=PS[bank][p0:p1, 0:n], func=AF.Ln,
                     bias=CEPS[p0:p1, 0:1], scale=inv_n),
              reads=[("ps", bank), "ceps"], writes=["t1"])
        S.add(ACT, I("activation", out=dst[p0:p1, 0:n], in_=T1[p0:p1, 0:n], func=AF.Exp, scale=-0.5),
              reads=["t1"], writes=[dst_tok])

    def norm_mod(l, j, avec, shift_base):
        for t in range(2):
            ts = slice(t * TT, (t + 1) * TT)
            bank = nbank()
            for c in range(NCH):
                k = nsq()
                S.add(ACT, I("activation", out=SQ[k], in_=X[:, c, ts], func=AF.Square),
                      reads=[("x", c, t)], writes=[("sq", k)])
                S.add(PE, I("matmul", PS[bank][:, :], ONES, SQ[k], start=(c == 0), stop=(c == NCH - 1)),
                      reads=[("sq", k), "ones"], writes=[("ps", bank)])
            rstd_from_bank(bank, (0, 128), TT, 1.0 / D_MODEL, RSTD, "rstd")
            for c in range(NCH):
                tb, tk = (T0, "t0") if c % 2 == 0 else (T1, "t1")
                S.add(DVE, I("scalar_tensor_tensor",
                    out=tb, in0=X[:, c, ts], scalar=VEC[:, avec, c:c + 1], in1=RSTD, op0=ALU.mult, op1=ALU.mult),
                    reads=[("x", c, t), "vec", "rstd"], writes=[tk])
                S.add(ACT, I("activation",
                    out=H[:, c, ts], in_=tb, func=AF.Identity, bias=MODT[l][:, shift_base + c, j:j + 1], scale=1.0),
                    reads=[tk, ("modt", l)], writes=[("h", c, t)])

    def hreads(t):
        return [("h", c, t) for c in range(NCH)]

    def proj_fm(l, blk, ncols=128):
        k = WIN_IDX[blk]
        tiles = [wload(w_in[l, 2 * k + g]) for g in range(2)]
        banks = []
        for t in range(2):
            bank = nbank()
            pairs = [(tiles[kc // 8][0][:, (kc % 8) * 128:(kc % 8) * 128 + ncols], H[:, kc, t * TT:(t + 1) * TT])
                     for kc in range(NCH)]
            mm_group(bank, (0, ncols), (0, TT), pairs, hreads(t) + [tl[1] for tl in tiles])
            banks.append(bank)
        return banks

    def proj_tm(l, blk, vdst, vtok, wcol, outd, ph):
        k = WIN_IDX[blk]
        tiles = [wload(w_in[l, 2 * k + g]) for g in range(2)]
        koff = 4 if ph == "S" else 0
        for half in range(2):
            bank = nbank()
            for q in range(4):
                tt = half * 4 + q
                t = tt // 4
                pairs = [(H[:, kc, tt * 128:(tt + 1) * 128], tiles[kc // 8][0][:, (kc % 8) * 128:(kc % 8) * 128 + 128])
                         for kc in range(NCH)]
                mm_group(bank, (0, 128), (q * 128, (q + 1) * 128), pairs, hreads(t) + [tl[1] for tl in tiles])
            S.add(ACT, I("activation",
                out=vdst[:, koff + half * 4:koff + half * 4 + 4, wcol:wcol + 128],
                in_=PS[bank][:, :].rearrange("p (q c) -> p q c", q=4), func=AF.Identity),
                reads=[("ps", bank)], writes=[vtok])
            if ph == "P":
                tb, tk = nostg()
                S.add(ACT, I("activation", out=tb, in_=PS[bank][:, :], func=AF.Identity),
                      reads=[("ps", bank)], writes=[tk])
                store(outd[l, half * 512:(half + 1) * 512, wcol:wcol + 128].rearrange("(q p) c -> p q c", p=128),
                      tb.rearrange("p (q c) -> p q c", q=4), tk)

    def evac_fm(bank, t, dst, dtok, ph, rope=None, np_=128, outd=None):
        ts = slice(t * TT, (t + 1) * TT)
        if rope is None:
            S.add(ACT, I("activation", out=dst, in_=PS[bank][0:np_, :], func=AF.Identity),
                  reads=[("ps", bank)], writes=[dtok])
        else:
            perm = PERM[rope]
            k = npst()
            S.add(ACT, I("activation", out=PST[k][0:np_, :], in_=PS[bank][0:np_, :], func=AF.Identity),
                  reads=[("ps", bank)], writes=[("pst", k)])
            b2 = nbank()
            mm_group(b2, (0, np_), (0, TT), [(perm[0:np_, 0:np_], PST[k][0:np_, :])], [("pst", k), "perm"])
            S.add(DVE, I("tensor_tensor", out=T0[0:np_, :], in0=PS[bank][0:np_, :], in1=ROPE_C[0:np_, ts], op=ALU.mult),
                  reads=[("ps", bank), "rope"], writes=["t0"])
            S.add(DVE, I("tensor_tensor", out=T1[0:np_, :], in0=PS[b2][0:np_, :], in1=ROPE_S[0:np_, ts], op=ALU.mult),
                  reads=[("ps", b2), "rope"], writes=["t1"])
            S.add(DVE, I("tensor_tensor", out=dst, in0=T0[0:np_, :], in1=T1[0:np_, :], op=ALU.add),
                  reads=["t0", "t1"], writes=[dtok])
        if outd is not None:
            ob_, ok_ = nostg()
            S.add(ACT, I("activation", out=ob_[0:np_, :], in_=PS[bank][0:np_, :], func=AF.Identity), reads=[("ps", bank)], writes=[ok_])
            store(outd[:, ts], ob_[0:np_, :], ok_)

    def load_rope(which):
        sp_load(ROPE_C, ropetab[2 * which], "rope")
        sp_load(ROPE_S, ropetab[2 * which + 1], "rope")

    acc_sel = [0]
    pend = []
    LOOKAHEAD = 2

    def attn_flush(keep=0):
        while len(pend) > keep:
            pend.pop(0)()

    def attn_unit(scale, nq, kblocks, extra_reads, banks=None, finish=None):
        if banks is None:
            ob, db = (4, 5) if acc_sel[0] % 2 == 0 else (6, 7)
            acc_sel[0] += 1
        else:
            ob, db = banks
        nb = len(kblocks)
        for i, kb in enumerate(kblocks):
            sb = st["sring"] % 3
            st["sring"] += 1
            mm_group(sb, (0, 128), (0, nq), kb["parts"], extra_reads)

            def proc(i=i, kb=kb, sb=sb):
                p0, p1 = kb["pr"]
                k = npst()
                S.add(ACT, I("activation", out=PST[k][p0:p1, 0:nq], in_=PS[sb][p0:p1, 0:nq], func=AF.Exp, scale=scale),
                      reads=[("ps", sb)], writes=[("pst", k)])
                if kb.get("mask") is not None:
                    S.add(DVE, I("tensor_tensor", out=PST[k][p0:p1, 0:nq], in0=PST[k][p0:p1, 0:nq], in1=kb["mask"], op=ALU.mult),
                          reads=[("pst", k)] + list(kb.get("mreads", [])), writes=[("pst", k)])

                def emit(e):
                    e.matmul(PS[ob][:, 0:nq], kb["v"], PST[k][p0:p1, 0:nq], start=(i == 0), stop=(i == nb - 1))
                    return e.matmul(PS[db][:, 0:nq], ONES[p0:p1, :], PST[k][p0:p1, 0:nq], start=(i == 0), stop=(i == nb - 1))
                S.add(PE, emit, reads=[("pst", k), "ones"] + list(extra_reads), writes=[("ps", ob), ("ps", db)])
                if i == nb - 1 and finish is not None:
                    finish(ob, db)
            pend.append(proc)
            attn_flush(LOOKAHEAD)
        return ob, db

    fin_cnt = [0]

    def fin_set(nq):
        ns = max(1, min(4, 512 // nq))
        i = fin_cnt[0] % ns
        fin_cnt[0] += 1
        c0 = i * nq
        gr = list(range(c0 // 64, (c0 + nq + 63) // 64))
        return slice(c0, c0 + nq), gr

    def attn_finish(ob, db, nq, dst, dtok, sink_col=None):
        cs, gr = fin_set(nq)
        t1g = [("t1g", g) for g in gr]
        t0g = [("t0g", g) for g in gr]
        if sink_col is not None:
            S.add(ACT, I("activation", out=T1[:, cs], in_=PS[db][:, 0:nq], func=AF.Ln, bias=sink_col, scale=1.0),
                  reads=[("ps", db), "misc"], writes=t1g)
        else:
            S.add(ACT, I("activation", out=T1[:, cs], in_=PS[db][:, 0:nq], func=AF.Ln),
                  reads=[("ps", db)], writes=t1g)
        S.add(ACT, I("activation", out=T0[:, cs], in_=T1[:, cs], func=AF.Exp, scale=-1.0), reads=t1g, writes=t0g)
        S.add(DVE, I("tensor_tensor", out=dst, in0=PS[ob][:, 0:nq], in1=T0[:, cs], op=ALU.mult),
              reads=[("ps", ob)] + t0g, writes=[dtok])

    def qkv_views(ph):
        nk = T + (512 if ph == "S" else 0)
        nkt = nk // 128
        return nk, nkt

    def mixer_A(l, ph):
        nk, nkt = qkv_views(ph)
        koff = nk - T
        o = [OFF_QKV]

        def al(nbytes, dt):
            v = view(o[0], nbytes, dt)
            o[0] += nbytes
            assert o[0] <= OFF_QKV + QKV_SZ
            return v
        UQ = al(6144, BF16)
        UKV = al(2048, BF16)
        CQ = al(6144, BF16).rearrange("p (c t) -> p c t", c=3)
        CKVT = al(nk * 2, BF16)
        KRT = al(nk * 2, BF16)
        QN = [al(2048, BF16) for _ in range(2)]
        QR = [al(2048, BF16) for _ in range(2)]
        KNT = [al(nk * 2, BF16) for _ in range(2)]
        VH = [al(nkt * 256, BF16).rearrange("p (k d) -> p k d", d=128) for _ in range(2)]
        pool_load(UQ, w_uq[l], "uq")
        pool_load(UKV, w_ukv[l], "ukv")
        if ph == "S":
            load_rope(1)
            pool_load(CKVT[:, 0:512], kc_mla[l], "ckvt")
            pool_load(KRT[0:64, 0:512], kc_kro[l], "krt")
        st["bank"] = 0
        qb = {}
        for c in range(3):
            k = WIN_IDX[("qdn", c)]
            tiles = [wload(w_in[l, 2 * k + g]) for g in range(2)]
            for t in range(2):
                bank = c * 2 + t
                pairs = [(tiles[kc // 8][0][:, (kc % 8) * 128:(kc % 8) * 128 + 128], H[:, kc, t * TT:(t + 1) * TT])
                         for kc in range(NCH)]
                mm_group(bank, (0, 128), (0, TT), pairs, hreads(t) + [tl[1] for tl in tiles])
                qb[(c, t)] = bank
        for t in range(2):
            sbk = 6 + t
            for c in range(3):
                k = nsq()
                S.add(ACT, I("activation", out=SQ[k], in_=PS[qb[(c, t)]][:, :], func=AF.Square),
                      reads=[("ps", qb[(c, t)])], writes=[("sq", k)])
                S.add(PE, I("matmul", PS[sbk][:, :], ONES, SQ[k], start=(c == 0), stop=(c == 2)),
                      reads=[("sq", k), "ones"], writes=[("ps", sbk)])
            rstd_from_bank(sbk, (0, 128), TT, 1.0 / 384, RSTD, "rstd")
            for c in range(3):
                S.add(DVE, I("scalar_tensor_tensor",
                    out=CQ[:, c, t * TT:(t + 1) * TT], in0=PS[qb[(c, t)]][:, :], scalar=SMALL[:, SM_QG + c:SM_QG + c + 1],
                    in1=RSTD, op0=ALU.mult, op1=ALU.mult),
                    reads=[("ps", qb[(c, t)]), "rstd", "small"], writes=[("cq", t)])
        cb = proj_fm(l, ("ckv", 0))
        for t in range(2):
            k = nsq()
            S.add(ACT, I("activation", out=SQ[k], in_=PS[cb[t]][:, :], func=AF.Square),
                  reads=[("ps", cb[t])], writes=[("sq", k)])
            sbk = nbank((4, 5, 6, 7))
            mm_group(sbk, (0, 128), (0, TT), [(ONES, SQ[k])], [("sq", k), "ones"])
            rstd_from_bank(sbk, (0, 128), TT, 1.0 / 128, RSTD, "rstd")
            S.add(DVE, I("scalar_tensor_tensor",
                out=T0, in0=PS[cb[t]][:, :], scalar=SMALL[:, SM_KVG:SM_KVG + 1], in1=RSTD, op0=ALU.mult, op1=ALU.mult),
                reads=[("ps", cb[t]), "rstd", "small"], writes=["t0"])
            S.add(ACT, I("activation", out=CKVT[:, koff + t * TT:koff + (t + 1) * TT], in_=T0, func=AF.Identity),
                  reads=["t0"], writes=["ckvt"])
            if ph == "P":
                store(o_ckv[l, :, t * TT:(t + 1) * TT], T0, "t0")
        kb_ = proj_fm(l, ("kro", 0), ncols=64)
        for t in range(2):
            evac_fm(kb_[t], t, KRT[0:64, koff + t * TT:koff + (t + 1) * TT], "krt", ph,
                    rope=(1 if ph == "S" else None), np_=64, outd=(o_kro[l] if ph == "P" else None))
        sc_a = (128 + 64) ** -0.5
        for h in range(4):
            s2 = h % 2
            attn_flush()
            for t in range(2):
                bank = nbank()
                pairs = [(UQ[:, h * 384 + kc * 128:h * 384 + (kc + 1) * 128], CQ[:, kc, t * TT:(t + 1) * TT]) for kc in range(3)]
                mm_group(bank, (0, 128), (0, TT), pairs, [("cq", t), "uq"])
                S.add(ACT, I("activation", out=QN[s2][:, t * TT:(t + 1) * TT], in_=PS[bank][:, :],
                                                                         func=AF.Identity),
                      reads=[("ps", bank)], writes=[("qn", s2)])
                bank = nbank()
                pairs = [(UQ[:, (4 + h) * 384 + kc * 128:(4 + h) * 384 + kc * 128 + 64], CQ[:, kc, t * TT:(t + 1) * TT])
                         for kc in range(3)]
                mm_group(bank, (0, 64), (0, TT), pairs, [("cq", t), "uq"])
                evac_fm(bank, t, QR[s2][0:64, t * TT:(t + 1) * TT], ("qr", s2), ph, rope=(1 if ph == "S" else None), np_=64)
            for kt in range(nk // TT):
                bank = nbank()
                mm_group(bank, (0, 128), (0, TT), [(UKV[:, h * 128:(h + 1) * 128], CKVT[:, kt * TT:(kt + 1) * TT])],
                         ["ckvt", "ukv"])
                S.add(ACT, I("activation", out=KNT[s2][:, kt * TT:(kt + 1) * TT], in_=PS[bank][:, :],
                                                                           func=AF.Identity),
                      reads=[("ps", bank)], writes=[("knt", s2)])
            for g in range(nkt // 4):
                bank = nbank()
                for q in range(4):
                    kk = g * 4 + q
                    mm_group(bank, (0, 128), (q * 128, (q + 1) * 128),
                             [(CKVT[:, kk * 128:(kk + 1) * 128], UKV[:, 512 + h * 128:512 + (h + 1) * 128])], ["ckvt", "ukv"])
                S.add(ACT, I("activation",
                    out=VH[s2][:, g * 4:(g + 1) * 4, :], in_=PS[bank][:, :].rearrange("p (q c) -> p q c", q=4), func=AF.Identity),
                    reads=[("ps", bank)], writes=[("vh", s2)])
            rd = [("qn", s2), ("qr", s2), ("knt", s2), ("vh", s2), "krt"]
            if ph == "P":
                for s in range(4):
                    q0 = s * 256
                    kbs = []
                    for j in range(2):
                        kc0 = q0 + j * 128
                        kbs.append(dict(parts=[(KNT[s2][:, kc0:kc0 + 128], QN[s2][:, q0:q0 + 256]),
                                               (KRT[0:64, kc0:kc0 + 128], QR[s2][0:64, q0:q0 + 256])],
                                        v=VH[s2][:, 2 * s + j, :], pr=(0, 128)))
                    attn_unit(sc_a, 256, kbs, rd,
                              finish=(lambda ob, db, a_=(256, OB[:, 0 + h, q0:q0 + 256], ("ob", 0 + h)): attn_finish(ob, db, *a_)))
            else:
                for t in range(2):
                    q0 = t * TT
                    kbs = []
                    for j in range(nkt):
                        kc0 = j * 128
                        kbs.append(dict(parts=[(KNT[s2][:, kc0:kc0 + 128], QN[s2][:, q0:q0 + TT]),
                                               (KRT[0:64, kc0:kc0 + 128], QR[s2][0:64, q0:q0 + TT])],
                                        v=VH[s2][:, j, :], pr=(0, 128)))
                    attn_unit(sc_a, TT, kbs, rd,
                              finish=(lambda ob, db, a_=(TT, OB[:, 0 + h, q0:q0 + TT], ("ob", 0 + h)): attn_finish(ob, db, *a_)))

    def proj_qk(l, ph, nm, nblk, dst_fn, tokname, rope, outd):
        for i in range(nblk):
            banks = proj_fm(l, (nm, i))
            for t in range(2):
                evac_fm(banks[t], t, dst_fn(i, t), (tokname, i), ph, rope=rope,
                        outd=(outd[l, i * 128:(i + 1) * 128, :] if (outd is not None and ph == "P" and not DBG_NOKOUT) else None))

    def mixer_BCD(l, ph, which):
        nk, nkt = qkv_views(ph)
        koff = nk - T
        kofft = koff // 128
        o = [OFF_QKV]

        def al(nbytes, dt):
            v = view(o[0], nbytes, dt)
            o[0] += nbytes
            assert o[0] <= OFF_QKV + QKV_SZ
            return v
        nkh = 2 if which == "B" else 4
        vw = nkh * 128
        Q = al(8192, BF16).rearrange("p (h t) -> p h t", h=4)
        KT = al(nkh * nk * 2, BF16).rearrange("p (h t) -> p h t", h=nkh)
        V = al(nkt * vw * 2, BF16).rearrange("p (k c) -> p k c", k=nkt)
        qn, kn, vn = {"B": ("qb", "kb", "vb"), "C": ("qc", "kc", "vc"), "D": ("qd", "kd", "vd")}[which]
        kcs, vcs = {"B": (kc_b, vc_b), "C": (kc_c, vc_c), "D": (kc_d, vc_d)}[which]
        okd, ovd = {"B": (o_kb, o_vb), "C": (o_kc, o_vc), "D": (o_kd, o_vd)}[which]
        mi = {"B": 1, "C": 2, "D": 3}[which]
        rope = None
        if ph == "S":
            if which == "B":
                load_rope(0)
                rope = 0
            elif which == "D":
                load_rope(1)
                rope = 1
            for hh in range(nkh):
                pool_load(KT[:, hh, 0:512], kcs[l, hh * 128:(hh + 1) * 128, :], ("kt", hh))
            pool_load(V[:, 0:4, :], vcs[l].rearrange("(k p) c -> p k c", p=128), "v")
            if which == "C":
                for q in range(8):
                    tb, tk = (T0, "t0") if q % 2 == 0 else (T1, "t1")
                    sp_load(tb, nabias[l, :, q * 512:(q + 1) * 512], tk)
                    hh, i0 = q // 2, (q % 2) * 8
                    S.add(ACT, I("activation",
                        out=ETAB[:, hh, i0:i0 + 8, :], in_=tb.rearrange("p (i c) -> p i c", i=8), func=AF.Exp),
                        reads=[tk], writes=["etab"])
                sp_load(RSTD[:, 0:128], cmask[2], "rstd")
                S.add(DVE, I("tensor_tensor",
                    out=ETAB.rearrange("p h i c -> p (h i) c"), in0=ETAB.rearrange("p h i c -> p (h i) c"),
                    in1=RSTD[:, 0:64].unsqueeze(1).broadcast_to([128, 64, 64]), op=ALU.mult),
                    reads=["etab", "rstd"], writes=["etab"])
        proj_qk(l, ph, qn, 4, lambda i, t: Q[:, i, t * TT:(t + 1) * TT], "q", rope, None)
        stage(which + "_q")
        proj_qk(l, ph, kn, nkh, lambda i, t: KT[:, i, koff + t * TT:koff + (t + 1) * TT], "kt", rope, okd)
        stage(which + "_k")
        for i in range(nkh):
            proj_tm(l, (vn, i), V, "v", i * 128, ovd, ph)
        stage(which + "_v")
        qr = [("q", i) for i in range(4)]
        kr = [("kt", i) for i in range(nkh)]
        rd = qr + kr + ["v"]
        sc = 128 ** -0.5
        if which == "B":
            S.add(ACT, I("activation", out=MISC[:, 0:4], in_=SMALL[:, SM_SINK:SM_SINK + 4], func=AF.Exp),
                  reads=["small"], writes=["misc"])
        if which == "D":
            diff_lambda(l)
        for h in range(4):
            kh = h // 2 if which == "B" else h
            if which in ("B", "C"):
                sink = MISC[:, h:h + 1] if which == "B" else None
                if ph == "P":
                    for s in range(4):
                        q0 = s * 256
                        kbs = [dict(parts=[(KT[:, kh, q0 + j * 128:q0 + (j + 1) * 128], Q[:, h, q0:q0 + 256])],
                                    v=V[:, 2 * s + j, kh * 128:(kh + 1) * 128], pr=(0, 128)) for j in range(2)]
                        attn_unit(sc, 256, kbs, rd,
                                  finish=(lambda ob, db, a_=(256, OB[:, mi * 4 + h, q0:q0 + 256], ("ob", mi * 4 + h), sink): attn_finish(ob, db, *a_)))
                elif which == "B":
                    for n in range(8):
                        q0 = n * 128
                        qs = Q[:, h, q0:q0 + 128]
                        kbs = [dict(parts=[(KT[:, kh, j * 128:(j + 1) * 128], qs)], v=V[:, j, kh * 128:(kh + 1) * 128],
                                    pr=(0, 128)) for j in range(4)]
                        for kbk in (n - 1, n, n + 1):
                            if kbk < 0 or kbk > 7:
                                continue
                            m = None if kbk == n else (MASKS[:, 0, :] if kbk == n - 1 else MASKS[:, 1, :])
                            kbs.append(dict(parts=[(KT[:, kh, 512 + kbk * 128:512 + (kbk + 1) * 128], qs)],
                                            v=V[:, 4 + kbk, kh * 128:(kh + 1) * 128], pr=(0, 128), mask=m, mreads=["masks"]))
                        attn_unit(sc, 128, kbs, rd,
                                  finish=(lambda ob, db, a_=(128, OB[:, mi * 4 + h, q0:q0 + 128], ("ob", mi * 4 + h), sink): attn_finish(ob, db, *a_)))
                else:
                    for r in range(16):
                        q0 = r * 64
                        qs = Q[:, h, q0:q0 + 64]
                        kbs = [dict(parts=[(KT[:, kh, j * 128:(j + 1) * 128], qs)], v=V[:, j, kh * 128:(kh + 1) * 128],
                                    pr=(0, 128)) for j in range(4)]
                        rs = min(max(r - 4, 0), 8)
                        for m_ in range(rs // 2, (rs + 7) // 2 + 1):
                            lo = 64 if 2 * m_ < rs else 0
                            hi = 64 if 2 * m_ + 1 > rs + 7 else 128
                            idx = 2 * m_ - r + 7 + 1
                            kbs.append(dict(parts=[(KT[:, kh, 512 + m_ * 128:512 + (m_ + 1) * 128], qs)],
                                            v=V[lo:hi, 4 + m_, kh * 128:(kh + 1) * 128], pr=(lo, hi),
                                            mask=ETAB[lo:hi, h, idx, :], mreads=["etab"]))
                        attn_unit(sc, 64, kbs, rd,
                                  finish=(lambda ob, db, a_=(64, OB[:, mi * 4 + h, q0:q0 + 64], ("ob", mi * 4 + h), sink): attn_finish(ob, db, *a_)))
            else:
                scd = 64 ** -0.5
                units = [(s * 256, 256, [2 * s, 2 * s + 1]) for s in range(4)] if ph == "P" else \
                        [(t * TT, TT, list(range(nkt))) for t in range(2)]
                for (q0, nq, kts) in units:
                    for a in range(2):
                        pa = slice(a * 64, (a + 1) * 64)
                        kbs = [dict(parts=[(KT[pa, kh, j * 128:(j + 1) * 128], Q[pa, h, q0:q0 + nq])],
                                    v=V[:, j, kh * 128:(kh + 1) * 128], pr=(0, 128)) for j in kts]
                        fin = None
                        if a == 1:
                            fin = (lambda ob, db, nq=nq, dst=OB[:, mi * 4 + h, q0:q0 + nq], tok=("ob", mi * 4 + h):
                                   diff_finish(l, [(4, 5), (6, 7)], nq, dst, tok))
                        attn_unit(scd, nq, kbs, rd, banks=((4, 5) if a == 0 else (6, 7)), finish=fin)

    def diff_lambda(l):
        lam = lambda k: T1[:, 64 * k:64 * (k + 1)]
        sp_load(T1[:, 0:256], small[l, :, SM_LAM:SM_LAM + 256], "t1")
        S.add(DVE, I("tensor_tensor", out=T0[:, 0:64], in0=lam(0), in1=lam(1), op=ALU.mult), reads=["t1"], writes=["t0"])
        S.add(DVE, I("tensor_reduce", out=MISC[:, 4:5], in_=T0[:, 0:64], axis=mybir.AxisListType.X, op=ALU.add),
              reads=["t0"], writes=["misc"])
        S.add(DVE, I("tensor_tensor", out=T0[:, 0:64], in0=lam(2), in1=lam(3), op=ALU.mult), reads=["t1", "misc"], writes=["t0"])
        S.add(DVE, I("tensor_reduce", out=MISC[:, 5:6], in_=T0[:, 0:64], axis=mybir.AxisListType.X, op=ALU.add),
              reads=["t0"], writes=["misc"])
        S.add(ACT, I("activation", out=MISC[:, 6:8], in_=MISC[:, 4:6], func=AF.Exp), reads=["misc"], writes=["misc"])
        S.add(DVE, I("scalar_tensor_tensor", out=MISC[:, 8:9], in0=MISC[:, 7:8], scalar=-lambda_init(l), in1=MISC[:, 6:7],
                                                    op0=ALU.add, op1=ALU.subtract), reads=["misc"], writes=["misc"])
        S.add(DVE, I("tensor_scalar", out=MISC[:, 9:10], in0=SMALL[:, SM_SUBG:SM_SUBG + 1], scalar1=1.0 - lambda_init(l),
                                             scalar2=1.0, op0=ALU.mult, op1=ALU.mult), reads=["small", "misc"], writes=["misc"])

    def diff_finish(l, res, nq, dst, dtok):
        (o1, d1), (o2, d2) = res
        cs, gr = fin_set(nq)
        t0g = [("t0g", g) for g in gr]
        t1g = [("t1g", g) for g in gr]
        rsg = [("rsg", g) for g in gr]
        A0, A1, RS = T0[:, cs], T1[:, cs], RSTD[:, cs]
        S.add(ACT, I("activation", out=A0, in_=PS[d1][:, 0:nq], func=AF.Ln), reads=[("ps", d1)], writes=t0g)
        S.add(ACT, I("activation", out=A0, in_=A0, func=AF.Exp, scale=-1.0), reads=t0g, writes=t0g)
        S.add(DVE, I("tensor_tensor", out=A0, in0=PS[o1][:, 0:nq], in1=A0, op=ALU.mult), reads=[("ps", o1)] + t0g, writes=t0g)
        S.add(ACT, I("activation", out=A1, in_=PS[d2][:, 0:nq], func=AF.Ln), reads=[("ps", d2)], writes=t1g)
        S.add(ACT, I("activation", out=A1, in_=A1, func=AF.Exp, scale=-1.0), reads=t1g, writes=t1g)
        S.add(DVE, I("tensor_tensor", out=A1, in0=PS[o2][:, 0:nq], in1=A1, op=ALU.mult), reads=[("ps", o2)] + t1g, writes=t1g)
        S.add(DVE, I("scalar_tensor_tensor", out=A0, in0=A1, scalar=MISC[:, 8:9], in1=A0, op0=ALU.mult, op1=ALU.add),
              reads=t0g + t1g + ["misc"], writes=t0g)
        k = nsq()
        S.add(ACT, I("activation", out=SQ[k][:, 0:nq], in_=A0, func=AF.Square), reads=t0g, writes=[("sq", k)])
        sbk = 3
        mm_group(sbk, (0, 128), (0, nq), [(ONES, SQ[k][:, 0:nq])], [("sq", k), "ones"])
        S.add(ACT, I("activation", out=RS, in_=PS[sbk][:, 0:nq], func=AF.Ln, bias=CEPS[:, 0:1], scale=1.0 / 128),
              reads=[("ps", sbk), "ceps"], writes=rsg)
        S.add(ACT, I("activation", out=RS, in_=RS, func=AF.Exp, scale=-0.5), reads=rsg, writes=rsg)
        S.add(DVE, I("scalar_tensor_tensor", out=dst, in0=A0, scalar=MISC[:, 9:10], in1=RS, op0=ALU.mult, op1=ALU.mult),
              reads=t0g + rsg + ["misc"], writes=[dtok])

    def merge(l):
        ACC = [RSTD, view(OFF_SCR + 6144, 2048, F32)]
        acct = ["rstd", "acc1"]
        for n in range(16):
            bt = None
            for i in range(4):
                gt = [wload(w_gate[l, (n * 4 + i) * 2 + g]) for g in range(2)]
                if i % 2 == 0:
                    bt = wload(w_br[l, n * 2 + i // 2])
                for t in range(2):
                    ts = slice(t * TT, (t + 1) * TT)
                    ga = nbank()
                    pairs = [(gt[kc // 8][0][:, (kc % 8) * 128:(kc % 8 + 1) * 128], H[:, kc, ts]) for kc in range(NCH)]
                    mm_group(ga, (0, 128), (0, TT), pairs, hreads(t) + [x[1] for x in gt])
                    gb = nbank((4, 5, 6, 7))
                    pairs = [(bt[0][:, (i % 2) * 512 + kc * 128:(i % 2) * 512 + (kc + 1) * 128], OB[:, i * 4 + kc, ts]) for kc in range(4)]
                    mm_group(gb, (0, 128), (0, TT), pairs, [("ob", i * 4 + kc) for kc in range(4)] + [bt[1]])
                    tb, tk = (T0, "t0") if t == 0 else (T1, "t1")
                    S.add(ACT, I("activation", out=tb, in_=PS[ga][:, :], func=AF.Sigmoid), reads=[("ps", ga)], writes=[tk])
                    if i == 0:
                        S.add(DVE, I("tensor_tensor", out=ACC[t], in0=PS[gb][:, :], in1=tb, op=ALU.mult),
                              reads=[("ps", gb), tk], writes=[acct[t], ("sq", 0), ("sq", 1)] if t == 1 else [acct[t]])
                    else:
                        S.add(DVE, I("tensor_tensor", out=tb, in0=PS[gb][:, :], in1=tb, op=ALU.mult),
                              reads=[("ps", gb), tk], writes=[tk])
                        if i < 3:
                            S.add(DVE, I("tensor_tensor", out=ACC[t], in0=ACC[t], in1=tb, op=ALU.add),
                                  reads=[acct[t], tk], writes=[acct[t]])
                        else:
                            S.add(DVE, I("tensor_tensor", out=MIXB[:, n, ts], in0=ACC[t], in1=tb, op=ALU.add),
                                  reads=[acct[t], tk], writes=[("mix", n, t)] + ([("sq", 0), ("sq", 1)] if t == 1 else []))

    def out_norm_res(l, wsrc, ntile, kcs, src_fn, src_reads_fn, gvec):
        for t in range(2):
            ts = slice(t * TT, (t + 1) * TT)
            sbk = 7
            for n in range(16):
                bank = nbank((0, 1, 2, 3, 4, 5))
                kc = 0
                for g in range(ntile):
                    tl = wload(wsrc[l, n * ntile + g], kcs[g] * 128)
                    pairs = []
                    for kk in range(kcs[g]):
                        pairs.append((tl[0][:, kk * 128:(kk + 1) * 128], src_fn(kc, ts)))
                        kc += 1
                    mm_group(bank, (0, 128), (0, TT), pairs, src_reads_fn(t) + [tl[1]], first=(g == 0), last=(g == ntile - 1))
                S.add(ACT, I("activation", out=MO[:, n, :], in_=PS[bank][:, :], func=AF.Identity),
                      reads=[("ps", bank)], writes=[("h", n, 0), ("h", n, 1)])
                k = nsq()
                S.add(ACT, I("activation", out=SQ[k], in_=PS[bank][:, :], func=AF.Square),
                      reads=[("ps", bank)], writes=[("sq", k)])
                S.add(PE, I("matmul", PS[sbk][:, :], ONES, SQ[k], start=(n == 0), stop=(n == 15)),
                      reads=[("sq", k), "ones"], writes=[("ps", sbk)])
                if n == 0:
                    stage(f"o{t}_n0")
            stage(f"o{t}_mm")
            rstd_from_bank(sbk, (0, 128), TT, 1.0 / D_MODEL, RSTD, "rstd")
            stage(f"o{t}_rs")
            for n in range(16):
                tb, tk = (T0, "t0") if n % 2 == 0 else (T1, "t1")
                S.add(DVE, I("scalar_tensor_tensor",
                    out=tb, in0=MO[:, n, :], scalar=VEC[:, gvec, n:n + 1], in1=RSTD, op0=ALU.mult, op1=ALU.mult),
                    reads=[("h", n, 0), ("h", n, 1), "vec", "rstd"], writes=[tk])
                S.add(DVE, I("tensor_tensor", out=X[:, n, ts], in0=X[:, n, ts], in1=tb, op=ALU.add),
                      reads=[tk, ("x", n, t)], writes=[("x", n, t)])
                if n == 0:
                    stage(f"o{t}_x0")
            stage(f"o{t}_x")

    def ffn_gate_up(l, mod_next=None):
        if mod_next is not None:
            sp_load(RSTD[:, 0:96], small[mod_next, :, SM_BADA:SM_BADA + 96], "rstd")
        for jc in range(NFC):
            if mod_next is not None and jc < 24:
                compute_mod(mod_next, jbs=[jc], small_tile=RSTD[:, 0:96], small_tok="rstd",
                            rowtmp=view(OFF_SCR + 6144, 2048, F32), rowtoks=(("sq", 0), ("sq", 1)))
            gtl = [wload(w_gu[l, (jc * 2 + 0) * 2 + g]) for g in range(2)]
            utl = [wload(w_gu[l, (jc * 2 + 1) * 2 + g]) for g in range(2)]
            for t in range(2):
                ts = slice(t * TT, (t + 1) * TT)
                ga = nbank()
                pairs = [(gtl[kc // 8][0][:, (kc % 8) * 128:(kc % 8 + 1) * 128], H[:, kc, ts]) for kc in range(NCH)]
                mm_group(ga, (0, 128), (0, TT), pairs, hreads(t) + [x[1] for x in gtl])
                ub = nbank((4, 5, 6, 7))
                pairs = [(utl[kc // 8][0][:, (kc % 8) * 128:(kc % 8 + 1) * 128], H[:, kc, ts]) for kc in range(NCH)]
                mm_group(ub, (0, 128), (0, TT), pairs, hreads(t) + [x[1] for x in utl])
                tb, tk = (T0, "t0") if t == 0 else (T1, "t1")
                S.add(ACT, I("activation", out=tb, in_=PS[ga][:, :], func=AF.Silu),
                      reads=[("ps", ga)], writes=[tk])
                S.add(DVE, I("tensor_tensor", out=ACTB[:, jc, ts], in0=PS[ub][:, :], in1=tb, op=ALU.mult),
                      reads=[("ps", ub), tk], writes=[("actb", jc, t)])

    class _Stop(Exception):
        pass

    def stage(name):
        if stop == name:
            raise _Stop()

    def run_all():
        for ph in phases:
            j = 0 if ph == "P" else 1
            for c in range(NCH):
                for t in range(2):
                    sp_load(X[:, c, t * TT:(t + 1) * TT], xin[ph][c * 128:(c + 1) * 128, t * TT:(t + 1) * TT], ("x", c, t))
            stage("load")
            for l in layers:
                load_small(l)
                if ph == phases[0] and l == layers[0]:
                    compute_mod(l)
                stage("mod")
                layer_vectors(l, j)
                norm_mod(l, j, 0, 0)
                stage("norm")
                attn_flush()
                S.barrier()
                mixer_A(l, ph)
                stage("A")
                attn_flush()
                S.barrier()
                mixer_BCD(l, ph, "B")
                stage("B")
                attn_flush()
                S.barrier()
                mixer_BCD(l, ph, "C")
                stage("C")
                attn_flush()
                S.barrier()
                mixer_BCD(l, ph, "D")
                stage("D")
                attn_flush()
                S.barrier()
                merge(l)
                stage("merge")
                out_norm_res(l, w_out, 2, (8, 8), lambda kc, ts: MIXB[:, kc, ts],
                             lambda t: [("mix", n, t) for n in range(16)], 1)
                stage("out")
                norm_mod(l, j, 2, 48)
                attn_flush()
                S.barrier()
                nxt = layers[layers.index(l) + 1] if (ph == phases[0] and layers.index(l) + 1 < len(layers)) else None
                ffn_gate_up(l, mod_next=nxt)
                stage("gu")
                out_norm_res(l, w_dn, 6, (8, 8, 8, 8, 8, 4), lambda kc, ts: ACTB[:, kc, ts],
                             lambda t: [("actb", jc, t) for jc in range(NFC)], 3)
            for c in range(NCH):
                for t in range(2):
                    store(yout[ph][c * 128:(c + 1) * 128, t * TT:(t + 1) * TT], X[:, c, t * TT:(t + 1) * TT], ("x", c, t))

    try:
        run_all()
    except _Stop:
        for c in range(NCH):
            for t in range(1 if os.environ.get('DBG_HALFSTORE') else 2):
                store(yout[phases[0]][c * 128:(c + 1) * 128, t * TT:(t + 1) * TT], X[:, c, t * TT:(t + 1) * TT], ("x", c, t))
    S.emit_all(nc, final_wait_ops=out_ops)
    return nc


def _tile_w(w, col_blocks, kcs):
    K = w.shape[0]
    out = []
    for (c0, ncols) in col_blocks:
        k0 = 0
        for kc in kcs:
            tl = np.zeros((128, 8, 128), np.float32)
            blk = w[k0 * 128:(k0 + kc) * 128, c0:c0 + ncols].reshape(kc, 128, ncols).transpose(1, 0, 2)
            tl[:, :kc, :ncols] = blk
            out.append(tl.reshape(128, 1024))
            k0 += kc
        assert k0 * 128 == K
    return np.stack(out)


def _prep_shared(inp):
    L = DEPTH
    sh = {}
    f = lambda a: np.ascontiguousarray(np.asarray(a, dtype=np.float32))
    w_ada = f(inp["w_ada"])
    t_ada = np.zeros((L, 24 * 8, 128, 1024), np.float32)
    for l in range(L):
        v = w_ada[l].reshape(8, 2, 128, 24, 512)
        t_ada[l] = v.transpose(3, 0, 2, 1, 4).reshape(24 * 8, 128, 1024)
    sh["w_ada_t"] = t_ada
    w_in = f(inp["w_in"])
    sh["w_in_t"] = np.stack([_tile_w(w_in[l], [(c0, nc_) for (_, _, c0, nc_) in WIN_BLOCKS], (8, 8)) for l in range(L)])
    w_uq = f(inp["w_mla_uq"])
    uq = np.zeros((L, 128, 8, 3, 128), np.float32)
    for l in range(L):
        for h in range(4):
            uq[l, :, h, :, :] = w_uq[l][:, h * 192:h * 192 + 128].reshape(3, 128, 128).transpose(1, 0, 2)
            uq[l, :, 4 + h, :, :64] = w_uq[l][:, h * 192 + 128:h * 192 + 192].reshape(3, 128, 64).transpose(1, 0, 2)
    sh["w_uq_t"] = uq.reshape(L, 128, 8 * 384)
    w_ukv = f(inp["w_mla_ukv"]).reshape(L, 128, 4, 2, 128)
    sh["w_ukv_t"] = np.ascontiguousarray(w_ukv.transpose(0, 1, 3, 2, 4)).reshape(L, 128, 1024)
    w_gate = f(inp["w_mix_gate"])
    sh["w_gate_t"] = np.stack([
        np.stack([_tile_w(w_gate[l, i], [(n * 128, 128) for n in range(16)], (8, 8)).reshape(16, 2, 128, 1024) for i in range(4)],
                 axis=1).reshape(16 * 4 * 2, 128, 1024) for l in range(L)])
    w_br = f(inp["w_branch"])
    br = np.zeros((L, 16, 2, 128, 2, 4, 128), np.float32)
    for l in range(L):
        for i in range(4):
            v = w_br[l, i].reshape(4, 128, 16, 128)
            br[l, :, i // 2, :, i % 2, :, :] = v.transpose(2, 1, 0, 3)
    sh["w_br_t"] = br.reshape(L, 32, 128, 1024)
    w_out = f(inp["w_out"])
    sh["w_out_t"] = np.stack([_tile_w(w_out[l], [(n * 128, 128) for n in range(16)], (8, 8)) for l in range(L)])
    wg, wu = f(inp["w_ffn_gate"]), f(inp["w_ffn_up"])
    gu = []
    for l in range(L):
        tg = _tile_w(wg[l], [(n * 128, 128) for n in range(NFC)], (8, 8)).reshape(NFC, 1, 2, 128, 1024)
        tu = _tile_w(wu[l], [(n * 128, 128) for n in range(NFC)], (8, 8)).reshape(NFC, 1, 2, 128, 1024)
        gu.append(np.concatenate([tg, tu], axis=1).reshape(NFC * 4, 128, 1024))
    sh["w_gu_t"] = np.stack(gu)
    wd = f(inp["w_ffn_down"])
    sh["w_dn_t"] = np.stack([_tile_w(wd[l], [(n * 128, 128) for n in range(16)], (8, 8, 8, 8, 8, 4)) for l in range(L)])
    sm = np.zeros((L, 128, NSMALL), np.float32)
    fm = lambda v, n: v.reshape(n, 128).T
    for l in range(L):
        for k, nm in enumerate(("mix_pre_g", "mix_post_g", "ffn_pre_g", "ffn_post_g")):
            sm[l, :, SM_GAIN + 16 * k:SM_GAIN + 16 * (k + 1)] = fm(f(inp[nm])[l], 16)
        sm[l, :, SM_BADA:SM_BADA + 96] = fm(f(inp["b_ada"])[l], 96)
        sm[l, :, SM_QG:SM_QG + 3] = fm(f(inp["mla_q_norm_g"])[l], 3)
        sm[l, :, SM_KVG] = f(inp["mla_kv_norm_g"])[l]
        sm[l, :, SM_SUBG] = f(inp["diff_subln_g"])[l]
        sm[l, :, SM_SINK:SM_SINK + 4] = f(inp["swa_sink"])[l][None, :]
        for k, nm in enumerate(("diff_lq1", "diff_lk1", "diff_lq2", "diff_lk2")):
            sm[l, :, SM_LAM + 64 * k:SM_LAM + 64 * (k + 1)] = f(inp[nm])[l][None, :]
    sh["small"] = sm
    p128 = np.zeros((128, 128), np.float32)
    p64 = np.zeros((128, 128), np.float32)
    for p in range(128):
        p128[p, (p + 64) % 128] = 1.0
        p64[p, (p // 64) * 64 + (p % 64 + 32) % 64] = 1.0
    sh["cperm"] = np.stack([p128, p64])
    sh["ceye"] = np.eye(2, dtype=np.float32)
    tpos = np.arange(T)
    row = (tpos // 64).astype(np.float32)
    col = (tpos % 64).astype(np.float32)
    tabs = []
    for r in (128, 64):
        quarter = r // 4
        inv = (1.0 / (10000.0 ** (np.arange(quarter, dtype=np.float32) / quarter))).astype(np.float32)
        ang = np.concatenate([row[:, None] * inv, col[:, None] * inv], axis=-1)
        cs, sn = np.cos(ang).T.astype(np.float32), np.sin(ang).T.astype(np.float32)
        C = np.concatenate([cs, cs], axis=0)
        Sg = np.concatenate([-sn, sn], axis=0)
        reps = 128 // r
        tabs += [np.tile(C, (reps, 1)), np.tile(Sg, (reps, 1))]
    sh["ropetab"] = np.stack(tabs).astype(np.float32)
    jj = np.arange(128)[:, None]
    ii = np.arange(128)[None, :]
    maskL = (jj >= ii).astype(np.float32)
    maskU = (jj <= ii).astype(np.float32)
    ck = np.arange(64)[:, None]
    cq = np.arange(64)[None, :]
    cs_ = np.clip(cq - 8, 0, 48)
    colv = ((ck >= cs_) & (ck < cs_ + 16)).astype(np.float32)
    m3 = np.zeros((128, 128), np.float32)
    m3[:64, :64] = colv
    m3[64:, :64] = colv
    sh["cmask"] = np.stack([maskL, maskU, m3])
    rpb = f(inp["na_rpb"])
    dc = np.clip(ck - cq + 15, 0, 30)
    nab = np.zeros((L, 128, 4, 16, 64), np.float32)
    for i in range(16):
        if 0 <= i - 1 <= 14:
            nab[:, :64, :, i, :] = rpb[:, :, i - 1, :][:, :, dc].transpose(0, 2, 1, 3)
        if i <= 14:
            nab[:, 64:, :, i, :] = rpb[:, :, i, :][:, :, dc].transpose(0, 2, 1, 3)
    sh["nabias"] = nab.reshape(L, 128, 4 * 16 * 64)
    return sh


_PROG = {}


def prep_in_maps(inp, cores=range(8)):
    f = lambda a: np.ascontiguousarray(np.asarray(a, dtype=np.float32))
    sh = _prep_shared(inp)
    xp, xs = f(inp["x_prompt"]), f(inp["x_sample"])
    c, c_ctx = f(inp["c"]), f(inp["c_ctx"])
    caches = {k: f(inp[k]) for k in ("cache_mla_ckv", "cache_mla_krope", "cache_swa_k", "cache_swa_v", "cache_na_k",
                                     "cache_na_v", "cache_diff_k", "cache_diff_v")}
    in_maps = []
    for b in cores:
        m = dict(sh)
        m["xP"] = np.ascontiguousarray(xp[4 * b:4 * b + 4].reshape(T, D_MODEL).T)
        m["xS"] = np.ascontiguousarray(xs[b].T)
        cv = np.stack([c_ctx, c[b]], axis=-1).reshape(16, 128, 2).transpose(1, 0, 2).reshape(128, 32)
        m["cvec"] = np.ascontiguousarray(cv)
        m["kc_mla"] = np.ascontiguousarray(caches["cache_mla_ckv"][b].transpose(0, 2, 1))
        m["kc_kro"] = np.ascontiguousarray(caches["cache_mla_krope"][b].transpose(0, 2, 1))
        m["kc_b"] = np.ascontiguousarray(caches["cache_swa_k"][b].reshape(DEPTH, 512, 256).transpose(0, 2, 1))
        m["kc_c"] = np.ascontiguousarray(caches["cache_na_k"][b].reshape(DEPTH, 512, 512).transpose(0, 2, 1))
        m["kc_d"] = np.ascontiguousarray(caches["cache_diff_k"][b].reshape(DEPTH, 512, 512).transpose(0, 2, 1))
        m["vc_b"] = np.ascontiguousarray(caches["cache_swa_v"][b].reshape(DEPTH, 512, 256))
        m["vc_c"] = np.ascontiguousarray(caches["cache_na_v"][b].reshape(DEPTH, 512, 512))
        m["vc_d"] = np.ascontiguousarray(caches["cache_diff_v"][b].reshape(DEPTH, 512, 512))
        in_maps.append(m)
    return in_maps


def kernel(**inp):
    in_maps = prep_in_maps(inp)
    if "nc" not in _PROG:
        _PROG["nc"] = build_program()
    res = run_bass_kernel_spmd(_PROG["nc"], in_maps, core_ids=list(range(8)))
    R = res.results
    cat = lambda k: [np.asarray(R[b][k], dtype=np.float32) for b in range(8)]
    y = np.concatenate([r.T.reshape(4, 256, D_MODEL) for r in cat("yP")], axis=0)
    z = np.stack([r.T for r in cat("yS")], axis=0)

    def kout(key, heads, d):
        outs = []
        for r in cat(key):
            v = r.reshape(DEPTH, r.shape[1], 4, 256).transpose(2, 0, 3, 1)
            outs.append(v.reshape(4, DEPTH, 256, heads, d) if heads else v)
        return np.ascontiguousarray(np.concatenate(outs, axis=0))

    def vout(key, heads, d):
        outs = []
        for r in cat(key):
            v = r.reshape(DEPTH, 4, 256, heads, d).transpose(1, 0, 2, 3, 4)
            outs.append(v)
        return np.ascontiguousarray(np.concatenate(outs, axis=0))
    new_ckv = kout("o_ckv", 0, 128)
    new_kro = kout("o_kro", 0, 64)
    return (np.ascontiguousarray(y), np.ascontiguousarray(z), new_ckv, new_kro,
            kout("o_kb", 2, 128), vout("o_vb", 2, 128), kout("o_kc", 4, 128), vout("o_vc", 4, 128),
            kout("o_kd", 4, 128), vout("o_vd", 4, 128))
```
